# Optimizing a Trainium2 kernel written in Bass

```python
import jax, jax.numpy as jnp
from jax import lax
import numpy as np

D_MODEL = 1024
BATCH = 8
SEQ = 4096
DEPTH = 4

CHUNK = 128
GM_GROUPS = 8
GM_WIDTH = 512
GM_GROUP_DIM = GM_WIDTH // GM_GROUPS
N_HEADS = 8
N_KV_HEADS = 2
HEAD_DIM = 64
Q_PER_KV = N_HEADS // N_KV_HEADS
Q_WIDTH = N_HEADS * HEAD_DIM
KV_WIDTH = N_KV_HEADS * HEAD_DIM
WINDOW = 128
BLOCK = 128
N_BRANCH = 2
D_FF = ((-(-8 * D_MODEL // 3) + 255) // 256) * 256
IN_WIDTH = 2 * GM_WIDTH + Q_WIDTH + 2 * KV_WIDTH + N_BRANCH * D_MODEL
N_MOD = 6
EPS = 1e-6
NEG_INF = -1e30

kernel_name = "hybrid_gmlp_swa_alibi_adaln_encoder"


def rmsnorm(x, gain):
    xf = x.astype(jnp.float32)
    y = xf * lax.rsqrt(jnp.mean(xf * xf, axis=-1, keepdims=True) + EPS)
    return (y * gain.astype(jnp.float32)).astype(x.dtype)


def alibi_slopes():
    h = jnp.arange(1, N_HEADS + 1, dtype=jnp.float32)
    return jnp.exp2(-8.0 * h / N_HEADS)


def gmlp_branch(u, v, v_gain, w_s, b_s):
    b, s, _ = u.shape
    u = jax.nn.gelu(u)
    v = rmsnorm(jax.nn.gelu(v), v_gain)
    v = v.reshape(b, s // CHUNK, CHUNK, GM_GROUPS, GM_GROUP_DIM)
    mixed = jnp.einsum('gts,bnsgc->bntgc', w_s, v) + b_s.T[None, None, :, :, None]
    return u * mixed.reshape(b, s, GM_WIDTH)


def window_attention_branch(q, k, v, q_gain, k_gain, sink):
    b, s, _ = q.shape
    nb = s // BLOCK
    q = rmsnorm(q.reshape(b, s, N_HEADS, HEAD_DIM), q_gain)
    k = rmsnorm(k.reshape(b, s, N_KV_HEADS, HEAD_DIM), k_gain)
    v = v.reshape(b, s, N_KV_HEADS, HEAD_DIM)
    qb = q.reshape(b, nb, BLOCK, N_KV_HEADS, Q_PER_KV, HEAD_DIM)

    def band(t):
        tp = jnp.pad(t, ((0, 0), (BLOCK, BLOCK), (0, 0), (0, 0)))
        tp = tp.reshape(b, nb + 2, BLOCK, N_KV_HEADS, HEAD_DIM)
        return jnp.concatenate([tp[:, :-2], tp[:, 1:-1], tp[:, 2:]], axis=2)

    kb, vb = band(k), band(v)
    logits = jnp.einsum('bnqkgd,bnskd->bnkgqs', qb, kb).astype(jnp.float32) * (HEAD_DIM ** -0.5)

    qi = jnp.arange(BLOCK)[:, None]
    kj = jnp.arange(3 * BLOCK)[None, :]
    dist = jnp.abs(kj - BLOCK - qi)
    kpos = jnp.arange(nb)[:, None] * BLOCK - BLOCK + jnp.arange(3 * BLOCK)[None, :]
    valid = (dist <= WINDOW)[None] & ((kpos >= 0) & (kpos < s))[:, None, :]
    slopes = alibi_slopes().reshape(N_KV_HEADS, Q_PER_KV)
    bias = -slopes[:, :, None, None] * dist.astype(jnp.float32)[None, None]
    logits = jnp.where(valid[None, :, None, None], logits + bias[None, None], NEG_INF)

    sink_logit = jnp.broadcast_to(
        sink.astype(jnp.float32).reshape(1, 1, N_KV_HEADS, Q_PER_KV, 1, 1),
        logits.shape[:-1] + (1,))
    probs = jax.nn.softmax(jnp.concatenate([logits, sink_logit], axis=-1), axis=-1)[..., :-1]
    out = jnp.einsum('bnkgqs,bnskd->bnqkgd', probs.astype(vb.dtype), vb)
    return out.reshape(b, s, Q_WIDTH)


def hybrid_layer(x, c, w_ada, b_ada, norm1_g, w_in, gm_v_g, gm_w_s, gm_b_s,
                 q_norm_g, k_norm_g, attn_sink, w_a, w_b, w_o, norm2_g,
                 w_ffn_in, w_ffn_out):
    mod = jax.nn.silu(c) @ w_ada + b_ada
    sh1, sc1, gt1, sh2, sc2, gt2 = jnp.split(mod[:, None, :], N_MOD, axis=-1)

    h = rmsnorm(x, norm1_g) * (1.0 + sc1) + sh1
    z = h @ w_in
    cuts = np.cumsum([GM_WIDTH, GM_WIDTH, Q_WIDTH, KV_WIDTH, KV_WIDTH, D_MODEL]).tolist()
    u, v, q, k, va, gate_a, gate_b = jnp.split(z, cuts, axis=-1)
    a_out = gmlp_branch(u, v, gm_v_g, gm_w_s, gm_b_s) @ w_a
    b_out = window_attention_branch(q, k, va, q_norm_g, k_norm_g, attn_sink) @ w_b
    merged = jax.nn.sigmoid(gate_a) * a_out + jax.nn.sigmoid(gate_b) * b_out
    x = x + gt1 * (merged @ w_o)

    h2 = rmsnorm(x, norm2_g) * (1.0 + sc2) + sh2
    f_gate, f_up = jnp.split(h2 @ w_ffn_in, 2, axis=-1)
    x = x + gt2 * ((jax.nn.silu(f_gate) * f_up) @ w_ffn_out)
    return x


def setup_inputs(seed: int = 0) -> dict:
    key = jax.random.key(seed)
    ks = jax.random.split(key, 20)
    f32 = jnp.float32

    def nrm(k, shape, scale):
        return jax.random.normal(k, shape, f32) * scale

    def gain(k, shape):
        return 1.0 + 0.02 * jax.random.normal(k, shape, f32)

    L, D = DEPTH, D_MODEL
    return {
        "x": nrm(ks[0], (BATCH, SEQ, D), 1.0),
        "c": nrm(ks[1], (BATCH, D), 1.0),
        "w_ada": nrm(ks[2], (L, D, N_MOD * D), 0.5 * D ** -0.5),
        "b_ada": nrm(ks[3], (L, N_MOD * D), 0.02),
        "norm1_g": gain(ks[4], (L, D)),
        "w_in": nrm(ks[5], (L, D, IN_WIDTH), D ** -0.5),
        "gm_v_g": gain(ks[6], (L, GM_WIDTH)),
        "gm_w_s": nrm(ks[7], (L, GM_GROUPS, CHUNK, CHUNK), CHUNK ** -0.5),
        "gm_b_s": gain(ks[8], (L, GM_GROUPS, CHUNK)),
        "q_norm_g": gain(ks[9], (L, HEAD_DIM)),
        "k_norm_g": gain(ks[10], (L, HEAD_DIM)),
        "attn_sink": nrm(ks[11], (L, N_HEADS), 0.5),
        "w_a": nrm(ks[12], (L, GM_WIDTH, D), GM_WIDTH ** -0.5),
        "w_b": nrm(ks[13], (L, Q_WIDTH, D), Q_WIDTH ** -0.5),
        "w_o": nrm(ks[14], (L, D, D), D ** -0.5),
        "norm2_g": gain(ks[15], (L, D)),
        "w_ffn_in": nrm(ks[16], (L, D, 2 * D_FF), D ** -0.5),
        "w_ffn_out": nrm(ks[17], (L, D_FF, D), D_FF ** -0.5),
    }


def reference(x, c, w_ada, b_ada, norm1_g, w_in, gm_v_g, gm_w_s, gm_b_s,
              q_norm_g, k_norm_g, attn_sink, w_a, w_b, w_o, norm2_g,
              w_ffn_in, w_ffn_out):
    for l in range(DEPTH):
        x = hybrid_layer(x, c, w_ada[l], b_ada[l], norm1_g[l], w_in[l], gm_v_g[l],
                         gm_w_s[l], gm_b_s[l], q_norm_g[l], k_norm_g[l], attn_sink[l],
                         w_a[l], w_b[l], w_o[l], norm2_g[l], w_ffn_in[l], w_ffn_out[l])
    return x
```

```python
import numpy as np
from contextlib import ExitStack
import concourse.bass as bass
import concourse.mybir as mybir
from concourse.bass_utils import run_bass_kernel_spmd

F32 = mybir.dt.float32
BF16 = mybir.dt.bfloat16
AF = mybir.ActivationFunctionType
ALU = mybir.AluOpType
AX = mybir.AxisListType

D = 1024
KC = 8
DFF = 2816
NHC = 22
INW = 3840
U0, V0, Q0, K0, VA0, GA0, GB0 = 0, 512, 1024, 1536, 1664, 1792, 2816
EPS = 1e-6
NEG = -30000.0

ENGS = ("pe", "act", "dve", "pool", "sp")
SEM_CHUNK = 4000
DMA_POOL = 12


class Buf:
    __slots__ = ("name", "w", "r", "rd")

    def __init__(self, name=""):
        self.name = name
        self.w = None
        self.r = {}
        self.rd = []


class Op:
    __slots__ = ("eng", "idx", "fn", "deps", "signal", "ndma", "dsem", "dval", "sigidx", "name")


class Sched:
    def __init__(self):
        self.ops = {e: [] for e in ENGS}
        self.dma_count = {e: 0 for e in ENGS}
        self.dma_last = {}
        self.dma_cum = {}

    def op(self, eng, fn, reads=(), writes=(), ndma=0, name=None):
        o = Op()
        o.eng = eng
        o.fn = fn
        o.ndma = ndma
        o.signal = False
        o.name = name
        o.idx = len(self.ops[eng])
        o.dsem = None
        o.dval = 0
        o.sigidx = -1
        deps = []
        raw = set()
        for b in reads:
            if b.w is not None:
                deps.append(b.w)
                raw.add(id(b.w))
        for b in writes:
            if b.w is not None:
                deps.append(b.w)
            for d_ in b.r.values():
                deps.append(d_)
                raw.add(id(d_))
            deps.extend(b.rd)
        if ndma:
            slot = self.dma_count[eng] % DMA_POOL
            self.dma_count[eng] += 1
            key = (eng, slot)
            prev = self.dma_last.get(key)
            if prev is not None:
                deps.append(prev)
            self.dma_last[key] = o
            cum = self.dma_cum.get(key, 0) + 16 * ndma
            self.dma_cum[key] = cum
            o.dsem = key
            o.dval = cum
        out = []
        seen = set()
        for d in deps:
            if d is o or id(d) in seen:
                continue
            seen.add(id(d))
            if d.ndma == 0 and d.eng == eng and ndma == 0:
                if eng == "pe":
                    continue
                if id(d) not in raw:
                    continue
            out.append(d)
        o.deps = out
        for d in out:
            if d.ndma == 0:
                d.signal = True
        for b in reads:
            if ndma:
                b.rd.append(o)
            else:
                b.r[eng] = o
        for b in writes:
            b.w = o
            b.r = {}
            b.rd = []
        self.ops[eng].append(o)
        return o

    def barrier(self):
        lasts = [self.ops[e][-1] for e in ENGS if self.ops[e]]
        pend = list(self.dma_last.values())
        for e in ENGS:
            o = self.op(e, lambda eng: eng.nop(), name="barrier")
            for d in lasts + pend:
                if d is o or (d.ndma == 0 and d.eng == e):
                    continue
                if d not in o.deps:
                    o.deps.append(d)
                    if d.ndma == 0:
                        d.signal = True

    def emit(self, nc, stack):
        nsem = {}
        for e in ENGS:
            k = 0
            for o in self.ops[e]:
                if o.ndma == 0 and o.signal:
                    o.sigidx = k
                    k += 1
            nsem[e] = (k + SEM_CHUNK - 1) // SEM_CHUNK
        esems = {e: [stack.enter_context(nc.semaphore(f"s_{e}_{i}")) for i in range(nsem[e])] for e in ENGS}
        dsems = {key: stack.enter_context(nc.semaphore(f"d_{key[0]}_{key[1]}")) for key in self.dma_cum}
        block = stack.enter_context(nc.Block())

        def run(ename, eng):
            seen = {}
            for o in self.ops[ename]:
                for d in o.deps:
                    if d.ndma:
                        sem = dsems[d.dsem]
                        val = d.dval
                        key = ("d",) + d.dsem
                    else:
                        sem = esems[d.eng][d.sigidx // SEM_CHUNK]
                        val = d.sigidx % SEM_CHUNK + 1
                        key = ("e", d.eng, d.sigidx // SEM_CHUNK)
                    if seen.get(key, 0) >= val:
                        continue
                    seen[key] = val
                    eng.wait_ge(sem, val)
                r = o.fn(eng)
                if o.ndma:
                    rs = r if isinstance(r, (list, tuple)) else [r]
                    assert len(rs) == o.ndma, (o.name, len(rs), o.ndma)
                    for ins in rs:
                        ins.then_inc(dsems[o.dsem], 16)
                elif o.signal:
                    ins = r[-1] if isinstance(r, (list, tuple)) else r
                    ins.then_inc(esems[ename][o.sigidx // SEM_CHUNK], 1)

        @block.tensor
        def _(eng):
            run("pe", eng)

        @block.scalar
        def _(eng):
            run("act", eng)

        @block.vector
        def _(eng):
            run("dve", eng)

        @block.gpsimd
        def _(eng):
            run("pool", eng)

        @block.sync
        def _(eng):
            run("sp", eng)


class Ring:
    def __init__(self, items):
        self.items = items
        self.i = 0

    def next(self):
        it = self.items[self.i % len(self.items)]
        self.i += 1
        return it


def build_program(T, L):
    NT = T // 128
    GM = 2
    GF = 2
    nc = bass.Bass("TRN2", target_bir_lowering=False)

    def din(name, shape):
        return nc.dram_tensor(name, shape, F32, kind="ExternalInput").ap()

    x_d = din("x", [T, D])
    c_d = din("c", [1, D])
    w_ada_d = din("w_ada", [L, D, 6 * D])
    b_ada_d = din("b_ada", [L, 6 * D])
    norm1_d = din("norm1_g", [L, D])
    w_in_d = din("w_in", [L, D, INW])
    vg_d = din("gm_v_g", [L, 512])
    ws_d = din("gm_w_s", [L, 8, 128, 128])
    bs_d = din("gm_b_s", [L, 8, 128])
    qg_d = din("q_norm_g", [L, 64])
    kg_d = din("k_norm_g", [L, 64])
    sink_d = din("attn_sink", [L, 8])
    wa_d = din("w_a", [L, 512, D])
    wb_d = din("w_b", [L, 512, D])
    wo_d = din("w_o", [L, D, D])
    norm2_d = din("norm2_g", [L, D])
    wfi_d = din("w_ffn_in", [L, D, 2 * DFF])
    wfo_d = din("w_ffn_out", [L, DFF, D])
    abias_d = din("abias", [128, 3, 2, 512])
    out_d = nc.dram_tensor("out", [T, D], F32, kind="ExternalOutput").ap()
    xa_d = nc.dram_tensor("xa_scratch", [T, D], F32).ap()
    xb_d = nc.dram_tensor("xb_scratch", [T, D], F32).ap()
    mod_d = nc.dram_tensor("mod_scratch", [L, 6 * D], F32).ap()
    wfi_bf_d = nc.dram_tensor("wfi_bf_scratch", [L, D, 2 * DFF], BF16).ap()
    wfo_bf_d = nc.dram_tensor("wfo_bf_scratch", [L, DFF, D], BF16).ap()

    S = Sched()
    dram_x = {k: [Buf(k) for _ in range(NT)] for k in ("x", "xa", "xb", "out")}

    with ExitStack() as top:
        _cnt = [0]

        def sbt(st, name, shape, dt):
            _cnt[0] += 1
            return st.enter_context(nc.sbuf_tensor(f"{name}_{_cnt[0]}", shape, dt))

        pbanks = []
        for i in range(6):
            pbanks.append((top.enter_context(nc.psum_tensor(f"pb{i}", [128, 512], F32)), Buf(f"pb{i}")))
        pring = Ring(pbanks)
        ptbanks = []
        for i in range(2):
            ptbanks.append((top.enter_context(nc.psum_tensor(f"pt{i}", [128, 1024], BF16)), Buf(f"pt{i}")))
        ptring = Ring(ptbanks)

        ident = sbt(top, "ident", [128, 128], BF16)
        scT = sbt(top, "scT", [128, KC], BF16)
        cT = sbt(top, "cT", [128, KC], F32)
        Etab = sbt(top, "Etab", [128, 3, 2, 512], BF16)
        cneg = sbt(top, "cneg", [128, 8], F32)
        modT_all = sbt(top, "modT_all", [128, L, 6, KC], F32)
        g1T_all = sbt(top, "g1T_all", [128, L, KC], F32)
        g2T_all = sbt(top, "g2T_all", [128, L, KC], F32)
        one_f = sbt(top, "one_f", [1, 8], F32)
        scB = sbt(top, "scB", [128, KC, 128], BF16)
        identF = sbt(top, "identF", [128, 128], F32)
        dscr = sbt(top, "dscr", [128, 128], F32)
        B_scB, B_identF, B_dscr = Buf(), Buf(), Buf()
        B_modTa = [Buf() for _ in range(L)]
        B_gall, B_onef = Buf(), Buf()
        B_ident, B_scT, B_cT, B_E, B_cneg = Buf(), Buf(), Buf(), Buf(), Buf()

        S.op("pool", lambda e: e.memset(ident[:], 0.0), writes=[B_ident])
        S.op("pool", lambda e: e.affine_select(out=ident[:], in_=ident[:], pattern=[[-1, 128]],
                                               compare_op=ALU.not_equal, fill=1.0, base=0,
                                               channel_multiplier=1), reads=[B_ident], writes=[B_ident])
        S.op("pool", lambda e: e.memset(cneg[:], -0.5), writes=[B_cneg])
        S.op("pool", lambda e: e.memset(one_f[:], 1.0), writes=[B_onef])
        for l_ in range(L):
            S.op("act", (lambda a: lambda e: e.dma_start(out=g1T_all[:, a, :], in_=norm1_d[a, :].rearrange("(kc p) -> p kc", p=128),
                                                         allow_slow_non_contiguous=True))(l_), writes=[B_gall], ndma=1)
            S.op("act", (lambda a: lambda e: e.dma_start(out=g2T_all[:, a, :], in_=norm2_d[a, :].rearrange("(kc p) -> p kc", p=128),
                                                         allow_slow_non_contiguous=True))(l_), writes=[B_gall], ndma=1)
        S.op("sp", lambda e: e.dma_start(out=cT[:], in_=c_d[0, :].rearrange("(kc p) -> p kc", p=128),
                                         allow_slow_non_contiguous=True), writes=[B_cT], ndma=1)
        S.op("act", lambda e: e.activation(out=scT[:], in_=cT[:], func=AF.Silu), reads=[B_cT], writes=[B_scT])
        S.op("dve", lambda e: e.tensor_copy(out=scB[:], in_=scT[:].unsqueeze(2).to_broadcast([128, KC, 128])), reads=[B_scT], writes=[B_scB])
        S.op("dve", lambda e: e.tensor_copy(out=identF[:], in_=ident[:]), reads=[B_ident], writes=[B_identF])
        def dma(eng, out_ap, in_ap, reads, writes, slow=False):
            if slow:
                return S.op(eng, lambda e: e.dma_start(out=out_ap, in_=in_ap, allow_slow_non_contiguous=True),
                            reads=reads, writes=writes, ndma=1)
            return S.op(eng, lambda e: e.dma_start(out=out_ap, in_=in_ap), reads=reads, writes=writes, ndma=1)

        def mm(out_ap, lhsT, rhs, start, stop, reads, writes):
            return S.op("pe", lambda e: e.matmul(out_ap, lhsT=lhsT, rhs=rhs, start=start, stop=stop),
                        reads=reads, writes=writes)

        def tr(out_ap, in_ap, reads, writes):
            return S.op("pe", lambda e: e.transpose(out_ap, in_ap, ident[:]), reads=list(reads) + [B_ident], writes=writes)

        def act(out_ap, in_ap, func, reads, writes, scale=None, bias=None, accum=None):
            def f(e):
                kw = {}
                if scale is not None:
                    kw["scale"] = scale
                if bias is not None:
                    kw["bias"] = bias
                if accum is not None:
                    kw["accum_out"] = accum
                return e.activation(out=out_ap, in_=in_ap, func=func, **kw)
            return S.op("act", f, reads=reads, writes=writes)

        def tt(eng, out_ap, in0, in1, op, reads, writes):
            return S.op(eng, lambda e: e.tensor_tensor(out=out_ap, in0=in0, in1=in1, op=op), reads=reads, writes=writes)

        def ts(eng, out_ap, in0, s1, s2, op0, op1, reads, writes):
            if s2 is None:
                return S.op(eng, lambda e: e.tensor_scalar(out=out_ap, in0=in0, scalar1=s1, scalar2=None, op0=op0),
                            reads=reads, writes=writes)
            return S.op(eng, lambda e: e.tensor_scalar(out=out_ap, in0=in0, scalar1=s1, scalar2=s2, op0=op0, op1=op1),
                        reads=reads, writes=writes)

        def stt(eng, out_ap, in0, scalar, in1, op0, op1, reads, writes):
            return S.op(eng, lambda e: e.scalar_tensor_tensor(out=out_ap, in0=in0, scalar=scalar, in1=in1, op0=op0, op1=op1),
                        reads=reads, writes=writes)

        def cp(eng, out_ap, in_ap, reads, writes):
            return S.op(eng, lambda e: e.tensor_copy(out=out_ap, in_=in_ap), reads=reads, writes=writes)

        def rstd(out_ap, ss_ap, tmp_ap, n, w, B_ss, B_tmp, B_out):
            ts("dve", tmp_ap, ss_ap, 1.0 / n, EPS, ALU.mult, ALU.add, [B_ss], [B_tmp])
            tt("pool", out_ap, tmp_ap, cneg[:, 0:w], ALU.pow, [B_tmp, B_cneg], [B_out])

        B_modrow = [Buf() for _ in range(L)]
        B_wfi_bf = [[Buf() for _ in range(11)] for _ in range(L)]
        B_wfo_bf = [[Buf() for _ in range(2)] for _ in range(L)]
        MH = 256

        def mod_load(lay, j, adab_t, B_adab_t, brow_t, B_brow_t):
            dma("pool", adab_t[:], w_ada_d[lay, :, j * MH:(j + 1) * MH].rearrange("(kc p) n -> p kc n", p=128), [], [B_adab_t])
            dma("act", brow_t[:], b_ada_d[lay, j * MH:(j + 1) * MH].partition_broadcast(128), [], [B_brow_t])

        def mod_compute(lay, j, adab_t, B_adab_t, brow_t, B_brow_t, mrow_t, B_mrow_t):
            pb, B_pb = pring.next()
            for kc in range(KC):
                mm(pb[:, 0:MH], scB[:, kc, :], adab_t[:, kc, :], kc == 0, kc == KC - 1, [B_scB, B_adab_t], [B_pb])
            tt("dve", mrow_t[:], pb[:, 0:MH], brow_t[:], ALU.add, [B_pb, B_brow_t], [B_mrow_t])
            dma("sp", mod_d[lay:lay + 1, j * MH:(j + 1) * MH], mrow_t[0:1, :], [B_mrow_t], [B_modrow[lay]])

        def mod_tr(lay, j, mrow_t, B_mrow_t):
            nseg = MH // 128
            col = j * MH
            m_, kc_ = col // D, (col % D) // 128
            for sg in range(nseg):
                tt("dve", dscr[:], mrow_t[:, sg * 128:(sg + 1) * 128], identF[:], ALU.mult, [B_mrow_t, B_identF], [B_dscr])
                S.op("dve", (lambda o_, i_: lambda e: e.tensor_reduce(out=o_, in_=i_, axis=AX.X, op=ALU.add))(
                    modT_all[:, lay, m_, kc_ + sg:kc_ + sg + 1], dscr[:]), reads=[B_dscr], writes=[B_modTa[lay]])

        def mod_block(lay, j, adab_t, B_adab_t, brow_t, B_brow_t, mrow_t, B_mrow_t):
            mod_load(lay, j, adab_t, B_adab_t, brow_t, B_brow_t)
            mod_compute(lay, j, adab_t, B_adab_t, brow_t, B_brow_t, mrow_t, B_mrow_t)

        NMB = 6 * D // MH
        with ExitStack() as st1_:
            estage = sbt(st1_, "estage", [128, 3, 2, 512], F32)
            B_es = Buf()
            S.op("sp", lambda e: e.dma_start(out=estage[:], in_=abias_d), writes=[B_es], ndma=1)
            for ri_ in range(3):
                for hh_ in range(2):
                    S.op("act", (lambda a, b: lambda e: e.activation(out=Etab[:, a, b, :], in_=estage[:, a, b, :], func=AF.Exp))(ri_, hh_),
                         reads=[B_es], writes=[B_E])
            adab0 = [sbt(st1_, f"adab0_{i}", [128, KC, MH], BF16) for i in range(2)]
            brow0 = [sbt(st1_, f"brow0_{i}", [128, MH], F32) for i in range(2)]
            mrow0 = [sbt(st1_, f"mrow0_{i}", [128, MH], F32) for i in range(2)]
            Bq = [[Buf(), Buf(), Buf()] for _ in range(2)]
            mod_load(0, 0, adab0[0], Bq[0][0], brow0[0], Bq[0][1])
            for j in range(NMB):
                r = j % 2
                if j + 1 < NMB:
                    mod_load(0, j + 1, adab0[1 - r], Bq[1 - r][0], brow0[1 - r], Bq[1 - r][1])
                mod_compute(0, j, adab0[r], Bq[r][0], brow0[r], Bq[r][1], mrow0[r], Bq[r][2])
                if j >= 1:
                    mod_tr(0, j - 1, mrow0[1 - r], Bq[1 - r][2])
            mod_tr(0, NMB - 1, mrow0[(NMB - 1) % 2], Bq[(NMB - 1) % 2][2])
            S.barrier()

        for l in range(L):
            x_src = x_d if l == 0 else xb_d
            B_src = dram_x["x"] if l == 0 else dram_x["xb"]
            x_mid, B_mid = xa_d, dram_x["xa"]
            if l == L - 1:
                x_dst, B_dst = out_d, dram_x["out"]
            else:
                x_dst, B_dst = xb_d, dram_x["xb"]

            with ExitStack() as ph:
                win = sbt(ph, "win", [128, KC, INW], BF16)
                wa = sbt(ph, "wa", [128, 4, D], BF16)
                wb = sbt(ph, "wb", [128, 4, D], BF16)
                wo = sbt(ph, "wo", [128, KC, D], BF16)
                wsT = sbt(ph, "wsT", [128, 8, 128], BF16)
                modT = sbt(ph, "modT", [128, 6, KC], F32)
                g1T = sbt(ph, "g1T", [128, KC], F32)
                gm1 = sbt(ph, "gm1", [128, KC], F32)
                vgain = sbt(ph, "vgain", [128, 512], F32)
                gq = sbt(ph, "gq", [128, 64], F32)
                gk = sbt(ph, "gk", [128, 64], F32)
                gqk = sbt(ph, "gqk", [128, 64], F32)
                bsT = sbt(ph, "bsT", [128, 4, 128], F32)
                sinkE = sbt(ph, "sinkE", [128, 8], F32)
                B_win = [Buf() for _ in range(8)]
                B_wa, B_wb, B_wo, B_wsT, B_modT = Buf(), Buf(), Buf(), Buf(), Buf()
                B_g1T, B_gm1, B_vgain, B_gq, B_gk, B_gqk, B_bsT, B_sinkE, B_gt1b = (Buf() for _ in range(9))

                wsn = sbt(ph, "wsn", [128, 8, 128], BF16)
                gt1b = sbt(ph, "gt1b", [128, D], F32)
                B_wsn = Buf()
                cp("dve", modT[:], modT_all[:, l, :, :], [B_modTa[l]], [B_modT])
                stt("dve", gm1[:], modT[:, 1, :], 1.0, g1T_all[:, l, :], ALU.add, ALU.mult, [B_modT, B_gall], [B_gm1])
                dma("sp", gq[:], qg_d[l, :].partition_broadcast(128), [], [B_gq])
                dma("sp", gk[:], kg_d[l, :].partition_broadcast(128), [], [B_gk])
                stt("dve", gqk[:], gq[:], 0.125, gk[:], ALU.mult, ALU.mult, [B_gq, B_gk], [B_gqk])
                dma("sp", vgain[:], vg_d[l, :].partition_broadcast(128), [], [B_vgain])
                dma("sp", sinkE[:], sink_d[l, :].partition_broadcast(128), [], [B_sinkE])
                act(sinkE[:], sinkE[:], AF.Exp, [B_sinkE], [B_sinkE])
                for hh in range(2):
                    dma("sp", bsT[hh * 64:(hh + 1) * 64, :, :], bs_d[l, hh::2, :].unsqueeze(0).to_broadcast([64, 4, 128]), [], [B_bsT])
                dma("sp", gt1b[:], mod_d[l, 2 * D:3 * D].partition_broadcast(128), [B_modrow[l]], [B_gt1b])
                for j in (3, 1, 2, 0, 4, 5, 6, 7):
                    c0 = j * 512
                    c1 = min(INW, c0 + 512)
                    dma("pool", win[:, :, c0:c1], w_in_d[l, :, c0:c1].rearrange("(kc p) n -> p kc n", p=128), [], [B_win[j]])
                dma("pool", wsn[:], ws_d[l].rearrange("g t s -> t g s"), [], [B_wsn])
                dma("pool", wa[:], wa_d[l].rearrange("(kc p) n -> p kc n", p=128), [], [B_wa])
                dma("pool", wb[:], wb_d[l].rearrange("(kc p) n -> p kc n", p=128), [], [B_wb])
                dma("pool", wo[:], wo_d[l].rearrange("(kc p) n -> p kc n", p=128), [], [B_wo])

                def preconvert_ffn(part):
                    if part < 11:
                        j = part
                        dma("pool", wfi_bf_d[l, :, j * 512:(j + 1) * 512], wfi_d[l, :, j * 512:(j + 1) * 512], [], [B_wfi_bf[l][j]])
                    else:
                        hf = part - 11
                        dma("pool", wfo_bf_d[l, hf * 1408:(hf + 1) * 1408, :], wfo_d[l, hf * 1408:(hf + 1) * 1408, :], [], [B_wfo_bf[l][hf]])

                def setup_wsT():
                    pt, B_pt = ptring.next()
                    for g_ in range(8):
                        tr(pt[:, g_ * 128:(g_ + 1) * 128], wsn[:, g_, :], [B_wsn], [B_pt])
                    cp("dve", wsT[:], pt[:, 0:1024].rearrange("p (g t) -> p g t", t=128), [B_pt], [B_wsT])

                def setup_wo():
                    for kc in range(KC):
                        stt("dve", wo[:, kc, :], wo[:, kc, :], 0.5, gt1b[:], ALU.mult, ALU.mult, [B_wo, B_gt1b], [B_wo])

                NHT = 4
                W = GM * 128
                xt = Ring([(sbt(ph, f"xt{i}", [128, D], F32), Buf()) for i in range(2)])
                xs = Ring([(sbt(ph, f"xs{i}", [128, D], BF16), Buf()) for i in range(2)])
                hT = sbt(ph, "hT", [128, KC, NHT * 128], BF16)
                B_hT = [Buf() for _ in range(NHT)]
                uT = sbt(ph, "uT", [128, 4, W], BF16)
                B_uT = Buf()
                sgT = sbt(ph, "sgT", [128, 16, W], BF16)
                B_sgT = [Buf() for _ in range(16)]
                gv = Ring([(sbt(ph, f"gv{i}", [128, 512], BF16), Buf()) for i in range(2)])
                vn = Ring([(sbt(ph, f"vn{i}", [128, 512], BF16), Buf()) for i in range(2)])
                sq = Ring([(sbt(ph, f"sq{i}", [128, 512], F32), Buf()) for i in range(2)])
                qn = Ring([(sbt(ph, f"qn{i}", [128, 512], BF16), Buf()) for i in range(2)])
                ksq = Ring([(sbt(ph, f"ksq{i}", [128, 128], F32), Buf()) for i in range(2)])
                ktmp = Ring([(sbt(ph, f"ktmp{i}", [128, 128], F32), Buf()) for i in range(2)])
                kdup = Ring([(sbt(ph, f"kdup{i}", [128, 256], BF16), Buf()) for i in range(2)])
                qT = sbt(ph, "qT", [128, 4, W], BF16)
                B_qT = [Buf() for _ in range(GM)]
                kT = sbt(ph, "kT", [128, 2, 8 * 128], BF16)
                B_kT = [Buf() for _ in range(8)]
                va = sbt(ph, "va", [128, 8, 2, 65], BF16)
                B_va = [Buf() for _ in range(8)]
                aT = sbt(ph, "aT", [128, 4, W], BF16)
                B_aT = [Buf() for _ in range(GM)]
                oT = sbt(ph, "oT", [128, 4, W], BF16)
                B_oT = [Buf() for _ in range(GM)]
                otok = Ring([(sbt(ph, f"otok{i}", [128, 512], BF16), Buf()) for i in range(2)])
                gtmp = Ring([(sbt(ph, f"gtmp{i}", [128, 512], F32), Buf()) for i in range(1)])
                expP = Ring([(sbt(ph, f"expP{i}", [128, 512], BF16), Buf()) for i in range(2)])
                PTr = Ring([(sbt(ph, f"PT{i}", [128, 3, 2, 512], BF16), [[Buf() for _ in range(2)] for _ in range(3)]) for i in range(2)])
                tmpA = sbt(ph, "tmpA", [128, KC, W], BF16)
                B_tmpA = [Buf() for _ in range(KC)]
                tmpB = Ring([(sbt(ph, f"tmpB{i}", [128, W], BF16), Buf()) for i in range(2)])
                mT = sbt(ph, "mT", [128, KC, W], BF16)
                B_mT = [Buf() for _ in range(KC)]
                xo = Ring([(sbt(ph, f"xo{i}", [128, D], F32), Buf()) for i in range(2)])
                NSL = 8
                st1 = sbt(ph, "st1", [128, NSL, 8], F32)
                stq = sbt(ph, "stq", [128, NSL, 3, 8], F32)
                stk = sbt(ph, "stk", [128, NSL, 3, 2], F32)
                dn = Ring([(sbt(ph, f"dn{i}", [128, 8], F32), Buf()) for i in range(2)])
                B_st = [[Buf() for _ in range(12)] for _ in range(NT)]
                B_st = [B_st[t_ % 8] for t_ in range(NT)]
                S.op("pool", (lambda v_: lambda e: e.memset(v_[:, :, :, 64:65], 1.0))(va), writes=B_va)

                e_state = {}

                def e_load(t):
                    xtt, B_xt = xt.next()
                    dma("sp", xtt[:], x_src[t * 128:(t + 1) * 128, :], [B_src[t]], [B_xt])
                    e_state[("x", t)] = (xtt, B_xt)

                def e_step_a(t):
                    if ("x", t) not in e_state:
                        e_load(t)
                    xtt, B_xt = e_state.pop(("x", t))
                    xst, B_xs = xs.next()
                    act(xst[:], xtt[:], AF.Square, [B_xt], [B_st[t][0], B_xs], accum=st1[:, t % 8, 0:1])
                    rstd(st1[:, t % 8, 2:3], st1[:, t % 8, 0:1], st1[:, t % 8, 1:2], D, 1, B_st[t][0], B_st[t][1], B_st[t][2])
                    ts("dve", xst[:], xtt[:], st1[:, t % 8, 2:3], None, ALU.mult, None, [B_xt, B_st[t][2]], [B_xs])
                    e_state[t] = (xst, B_xs)

                def e1(t):
                    s4 = t % NHT
                    xst, B_xs = e_state.pop(t)
                    pt, B_pt = ptring.next()
                    for kc in range(KC):
                        tr(pt[:, kc * 128:(kc + 1) * 128], xst[:, kc * 128:(kc + 1) * 128], [B_xs], [B_pt])
                    for kc in range(KC):
                        ts("dve", hT[:, kc, s4 * 128:(s4 + 1) * 128], pt[:, kc * 128:(kc + 1) * 128],
                           gm1[:, kc:kc + 1], modT[:, 0, kc:kc + 1], ALU.mult, ALU.add,
                           [B_pt, B_gm1, B_modT], [B_hT[s4]])

                def e2(t):
                    s4 = t % NHT
                    s8 = t % 8
                    pb, B_pb = pring.next()
                    for kc in range(KC):
                        mm(pb[:, 0:256], hT[:, kc, s4 * 128:(s4 + 1) * 128], win[:, kc, K0:K0 + 256], kc == 0, kc == KC - 1,
                           [B_hT[s4], B_win[3]], [B_pb])
                    act(va[:, s8, :, 0:64], pb[:, 128:256].rearrange("p (kv d) -> p kv d", d=64), AF.Copy, [B_pb], [B_va[s8]])
                    ks, B_ks = ksq.next()
                    act(ks[:], pb[:, 0:128], AF.Square, [B_pb], [B_ks])
                    S.op("dve", (lambda o_, i_: lambda e: e.tensor_reduce(out=o_, in_=i_, axis=AX.X, op=ALU.add))(
                        stk[:, t % 8, 0, :], ks[:].rearrange("p (h d) -> p h d", d=64)), reads=[B_ks], writes=[B_st[t][6]])
                    rstd(stk[:, t % 8, 2, :], stk[:, t % 8, 0, :], stk[:, t % 8, 1, :], 64, 2, B_st[t][6], B_st[t][7], B_st[t][8])
                    kt_, B_kt = ktmp.next()
                    tt("dve", kt_[:].rearrange("p (h d) -> p h d", d=64), pb[:, 0:128].rearrange("p (h d) -> p h d", d=64),
                       stk[:, t % 8, 2, :].unsqueeze(2).to_broadcast([128, 2, 64]), ALU.mult, [B_pb, B_st[t][8]], [B_kt])
                    kd, B_kd = kdup.next()
                    for dup in range(2):
                        tt("dve", kd[:].rearrange("p (kv u d) -> p kv u d", kv=2, u=2)[:, :, dup, :],
                           kt_[:].rearrange("p (h d) -> p h d", d=64),
                           gqk[:].unsqueeze(1).to_broadcast([128, 2, 64]), ALU.mult, [B_kt, B_gqk], [B_kd])
                    e_state[("k", t)] = (kd, B_kd)

                def e3(t):
                    s8 = t % 8
                    kd, B_kd = e_state.pop(("k", t))
                    pt2, B_pt2 = ptring.next()
                    for kv in range(2):
                        tr(pt2[:, kv * 128:(kv + 1) * 128], kd[:, kv * 128:(kv + 1) * 128], [B_kd], [B_pt2])
                    cp("dve", kT[:, :, s8 * 128:(s8 + 1) * 128], pt2[:, 0:256].rearrange("p (kv t) -> p kv t", t=128), [B_pt2], [B_kT[s8]])

                def tok_mm(t, i):
                    s4 = t % NHT
                    pb, B_pb = pring.next()
                    for kc in range(KC):
                        mm(pb[:], hT[:, kc, s4 * 128:(s4 + 1) * 128], win[:, kc, V0:V0 + 512], kc == 0, kc == KC - 1,
                           [B_hT[s4], B_win[1]], [B_pb])
                    gvt, B_gv = gv.next()
                    act(gvt[:], pb[:], AF.Gelu_apprx_tanh, [B_pb], [B_gv])
                    vnt, B_vn = vn.next()
                    act(vnt[:], gvt[:], AF.Square, [B_gv], [B_st[t][3], B_vn], accum=st1[:, t % 8, 3:4])
                    rstd(st1[:, t % 8, 5:6], st1[:, t % 8, 3:4], st1[:, t % 8, 4:5], 512, 1, B_st[t][3], B_st[t][4], B_st[t][5])
                    stt("dve", vnt[:], gvt[:], st1[:, t % 8, 5:6], vgain[:], ALU.mult, ALU.mult, [B_gv, B_st[t][5], B_vgain], [B_vn])
                    pq, B_pq = pring.next()
                    for kc in range(KC):
                        mm(pq[:], hT[:, kc, s4 * 128:(s4 + 1) * 128], win[:, kc, Q0:Q0 + 512], kc == 0, kc == KC - 1,
                           [B_hT[s4], B_win[2]], [B_pq])
                    sqt, B_sq = sq.next()
                    act(sqt[:], pq[:], AF.Square, [B_pq], [B_sq])
                    S.op("dve", (lambda o_, i_: lambda e: e.tensor_reduce(out=o_, in_=i_, axis=AX.X, op=ALU.add))(
                        stq[:, t % 8, 0, :], sqt[:].rearrange("p (h d) -> p h d", d=64)), reads=[B_sq], writes=[B_st[t][9]])
                    rstd(stq[:, t % 8, 2, :], stq[:, t % 8, 0, :], stq[:, t % 8, 1, :], 64, 8, B_st[t][9], B_st[t][10], B_st[t][11])
                    qnt, B_qn = qn.next()
                    tt("dve", qnt[:].rearrange("p (h d) -> p h d", d=64), pq[:].rearrange("p (h d) -> p h d", d=64),
                       stq[:, t % 8, 2, :].unsqueeze(2).to_broadcast([128, 8, 64]), ALU.mult, [B_pq, B_st[t][11]], [B_qn])
                    return (vnt, B_vn, qnt, B_qn)

                def q_tr(i, qnt, B_qn):
                    pt, B_pt = ptring.next()
                    for j in range(4):
                        tr(pt[:, j * 128:(j + 1) * 128], qnt[:, j * 128:(j + 1) * 128], [B_qn], [B_pt])
                    cp("dve", qT[:, :, i * 128:(i + 1) * 128], pt[:, 0:512].rearrange("p (j t) -> p j t", t=128), [B_pt], [B_qT[i]])

                def gmlp(i, vnt, B_vn):
                    pg, B_pg = pring.next()
                    for j in range(4):
                        for hh in range(2):
                            g = 2 * j + hh
                            mm(pg[hh * 64:(hh + 1) * 64, j * 128:(j + 1) * 128], vnt[:, g * 64:(g + 1) * 64], wsT[:, g, :],
                               True, True, [B_vn, B_wsT], [B_pg])
                    gt_, B_gt = gtmp.next()
                    tt("dve", gt_[:], pg[:], bsT[:].rearrange("p j t -> p (j t)"), ALU.add, [B_pg, B_bsT], [B_gt])
                    tt("dve", aT[:, :, i * 128:(i + 1) * 128], gt_[:].rearrange("p (j t) -> p j t", t=128),
                       uT[:, :, i * 128:(i + 1) * 128], ALU.mult, [B_gt, B_uT], [B_aT[i]])

                def attn_logits(t, i, filler=(), pre_last=None):
                    rels = [r_ for r_ in (-1, 0, 1) if 0 <= t + r_ < NT]
                    PT, B_PT = PTr.next()
                    filler = list(filler)
                    for r_ in rels:
                        if r_ != rels[0] and filler:
                            c_wa(*filler.pop(0))
                        if r_ == 1 and pre_last is not None:
                            pre_last()
                        ri = r_ + 1
                        s8r = (t + r_) % 8
                        for hh in range(2):
                            pb, B_pb = pring.next()
                            for kv in range(2):
                                mm(pb[:, kv * 256:(kv + 1) * 256].rearrange("p (j t) -> p j t", t=128),
                                   kT[hh * 64:(hh + 1) * 64, kv, s8r * 128:(s8r + 1) * 128],
                                   qT[hh * 64:(hh + 1) * 64, kv * 2:kv * 2 + 2, i * 128:(i + 1) * 128],
                                   True, True, [B_kT[s8r], B_qT[i]], [B_pb])
                            ex, B_ex = expP.next()
                            act(ex[:], pb[:], AF.Exp, [B_pb], [B_ex])
                            tt("dve", PT[:, ri, hh, :], ex[:], Etab[:, ri, hh, :], ALU.mult, [B_ex, B_E], [B_PT[ri][hh]])
                    while filler:
                        c_wa(*filler.pop(0))
                    return (rels, PT, B_PT)

                def attn_pv(t, i, rels, PT, B_PT):
                    banks = [pring.next(), pring.next()]
                    for h in range(8):
                        j, hh = divmod(h, 2)
                        kv = j // 2
                        po, B_po = banks[h // 4]
                        c0 = (h % 4) * 65
                        for idx, r_ in enumerate(rels):
                            ri = r_ + 1
                            s8r = (t + r_) % 8
                            mm(po[:, c0:c0 + 65], PT[:, ri, hh, j * 128:(j + 1) * 128], va[:, s8r, kv, :],
                               idx == 0, idx == len(rels) - 1, [B_PT[ri][hh], B_va[s8r]], [B_po])
                    dnt, B_dn = dn.next()
                    for bi in range(2):
                        po, B_po = banks[bi]
                        tt("dve", dnt[:, bi * 4:(bi + 1) * 4].unsqueeze(2),
                           po[:, 0:260].rearrange("p (h e) -> p h e", e=65)[:, :, 64:65],
                           sinkE[:, bi * 4:(bi + 1) * 4].unsqueeze(2), ALU.add, [B_po, B_sinkE], [B_dn])
                    S.op("dve", lambda e: e.reciprocal(out=dnt[:], in_=dnt[:]), reads=[B_dn], writes=[B_dn])
                    ot, B_ot = otok.next()
                    for bi in range(2):
                        po, B_po = banks[bi]
                        tt("dve", ot[:, bi * 256:(bi + 1) * 256].rearrange("p (h d) -> p h d", d=64),
                           po[:, 0:260].rearrange("p (h e) -> p h e", e=65)[:, :, 0:64],
                           dnt[:, bi * 4:(bi + 1) * 4].unsqueeze(2).to_broadcast([128, 4, 64]), ALU.mult,
                           [B_po, B_dn], [B_ot])
                    return (ot, B_ot)

                def attn_ot(i, ot, B_ot):
                    pt, B_pt = ptring.next()
                    for j in range(4):
                        tr(pt[:, j * 128:(j + 1) * 128], ot[:, j * 128:(j + 1) * 128], [B_ot], [B_pt])
                    cp("dve", oT[:, :, i * 128:(i + 1) * 128], pt[:, 0:512].rearrange("p (j t) -> p j t", t=128), [B_pt], [B_oT[i]])

                def feat_chunk(g, kind, cc):
                    t0 = g * GM
                    s0 = (t0 % NHT) * 128
                    rb = [B_hT[(t0 + i) % NHT] for i in range(GM)]
                    c0 = (U0 if kind == "u" else GA0) + cc * 128
                    pb, B_pb = pring.next()
                    for kc in range(KC):
                        mm(pb[:, 0:W], win[:, kc, c0:c0 + 128], hT[:, kc, s0:s0 + W],
                           kc == 0, kc == KC - 1, rb + [B_win[c0 // 512]], [B_pb])
                    if kind == "u":
                        act(uT[:, cc, :], pb[:, 0:W], AF.Gelu_apprx_tanh, [B_pb], [B_uT])
                    else:
                        act(sgT[:, cc, :], pb[:, 0:W], AF.Tanh, [B_pb], [B_sgT[cc]], scale=0.5)

                def c_wa(d0, d1):
                    for dc in range(d0, d1):
                        pa, B_pa = pring.next()
                        for cc in range(4):
                            mm(pa[:, 0:W], wa[:, cc, dc * 128:(dc + 1) * 128], aT[:, cc, :], cc == 0, cc == 3, B_aT + [B_wa], [B_pa])
                        stt("dve", tmpA[:, dc, :], sgT[:, dc, :], 1.0, pa[:, 0:W], ALU.add, ALU.mult, [B_pa, B_sgT[dc]], [B_tmpA[dc]])

                def c_wb():
                    for dc in range(KC):
                        pb_, B_pb_ = pring.next()
                        for j in range(4):
                            mm(pb_[:, 0:W], wb[:, j, dc * 128:(dc + 1) * 128], oT[:, j, :], j == 0, j == 3, B_oT + [B_wb], [B_pb_])
                        tb, B_tb = tmpB.next()
                        stt("dve", tb[:], sgT[:, 8 + dc, :], 1.0, pb_[:, 0:W], ALU.add, ALU.mult, [B_pb_, B_sgT[8 + dc]], [B_tb])
                        tt("pool", mT[:, dc, :], tmpA[:, dc, :], tb[:], ALU.add, [B_tmpA[dc], B_tb], [B_mT[dc]])

                def c_wo(g):
                    t0 = g * GM
                    loads = []
                    for i in range(GM):
                        t = t0 + i
                        xot, B_xo = xo.next()
                        dma("sp", xot[:], x_src[t * 128:(t + 1) * 128, :], [B_src[t]], [B_xo])
                        loads.append((xot, B_xo))
                    for i in range(GM):
                        t = t0 + i
                        xot, B_xo = loads[i]
                        for hc in range(2):
                            po, B_po = pring.next()
                            for dc in range(KC):
                                mm(po[:], mT[:, dc, i * 128:(i + 1) * 128], wo[:, dc, hc * 512:(hc + 1) * 512], dc == 0, dc == KC - 1,
                                   B_mT + [B_wo], [B_po])
                            tt("dve", xot[:, hc * 512:(hc + 1) * 512], po[:], xot[:, hc * 512:(hc + 1) * 512], ALU.add,
                               [B_po, B_xo], [B_xo])
                        dma("sp", x_mid[t * 128:(t + 1) * 128, :], xot[:], [B_xo], [B_mid[t]])

                NG = NT // GM
                for t in range(GM):
                    e_step_a(t)
                    e1(t)
                    e2(t)
                    e3(t)
                toks = [tok_mm(i, i) for i in range(GM)]
                for cc in range(4):
                    feat_chunk(0, "u", cc)
                for g in range(NG):
                    t0 = g * GM
                    nxt = [t for t in (t0 + GM, t0 + GM + 1) if t < NT]
                    n0 = nxt[0] if len(nxt) > 0 else None
                    n1 = nxt[1] if len(nxt) > 1 else None
                    for t in nxt:
                        e_step_a(t)
                    for cc in range(8):
                        feat_chunk(g, "g", cc)
                    for i in range(GM):
                        q_tr(i, toks[i][2], toks[i][3])
                    if n0 is not None:
                        e1(n0)
                    for cc in range(8, 16):
                        feat_chunk(g, "g", cc)
                    if g == 0:
                        setup_wsT()
                    for i in range(GM):
                        gmlp(i, toks[i][0], toks[i][1])
                    if n0 is not None:
                        e2(n0)
                    if n1 is not None:
                        e1(n1)
                    lg0 = attn_logits(t0, 0, filler=[(0, 2), (2, 4)])
                    lg1 = attn_logits(t0 + 1, 1, filler=[(4, 6), (6, 8)],
                                      pre_last=(lambda n_=n0: e3(n_)) if n0 is not None else None)
                    if n1 is not None:
                        e2(n1)
                    o0 = attn_pv(t0, 0, *lg0)
                    o1 = attn_pv(t0 + 1, 1, *lg1)
                    for t_ in (t0 + 2 * GM, t0 + 2 * GM + 1):
                        if t_ < NT:
                            e_load(t_)
                    ntoks = []
                    if n0 is not None:
                        ntoks.append(tok_mm(n0, 0))
                    attn_ot(0, *o0)
                    attn_ot(1, *o1)
                    if n1 is not None:
                        e3(n1)
                    c_wb()
                    if n1 is not None:
                        ntoks.append(tok_mm(n1, 1))
                    for part_ in range(13):
                        if (NG >= 15 and g == 1 + part_) or (NG < 15 and g == 0):
                            preconvert_ffn(part_)
                    if n0 is not None:
                        for cc in range(4):
                            feat_chunk(g + 1, "u", cc)
                    if g == 0:
                        setup_wo()
                    c_wo(g)
                    toks = ntoks
                S.barrier()

            with ExitStack() as ph:
                wfi = sbt(ph, "wfi", [128, KC, 2 * DFF], BF16)
                wfo = sbt(ph, "wfo", [128, NHC, D], BF16)
                modT2 = sbt(ph, "modT2", [128, 6, KC], F32)
                g2T = sbt(ph, "g2Tb", [128, KC], F32)
                gm2 = sbt(ph, "gm2", [128, KC], F32)
                gt2b = sbt(ph, "gt2b", [128, D], F32)
                B_wfi = [Buf() for _ in range(11)]
                B_wfo = [Buf() for _ in range(2)]
                B_modT2, B_g2T, B_gm2, B_gt2b = (Buf() for _ in range(4))
                cp("dve", modT2[:], modT_all[:, l, :, :], [B_modTa[l]], [B_modT2])
                stt("dve", gm2[:], modT2[:, 4, :], 1.0, g2T_all[:, l, :], ALU.add, ALU.mult, [B_modT2, B_gall], [B_gm2])
                dma("sp", gt2b[:], mod_d[l, 5 * D:6 * D].partition_broadcast(128), [B_modrow[l]], [B_gt2b])
                order = []
                for hcx in range(NHC):
                    for c_ in (hcx * 128, DFF + hcx * 128):
                        if c_ // 512 not in order:
                            order.append(c_ // 512)
                for j in order:
                    dma("pool", wfi[:, :, j * 512:(j + 1) * 512], wfi_bf_d[l, :, j * 512:(j + 1) * 512].rearrange("(kc p) n -> p kc n", p=128), [B_wfi_bf[l][j]], [B_wfi[j]])
                for hf in range(2):
                    dma("pool", wfo[:, hf * 11:(hf + 1) * 11, :], wfo_bf_d[l, hf * 11 * 128:(hf + 1) * 11 * 128, :].rearrange("(kc p) n -> p kc n", p=128), [B_wfo_bf[l][hf]], [B_wfo[hf]])

                def setup_wfo():
                    for kk in range(NHC):
                        tt("dve", wfo[:, kk, :], wfo[:, kk, :], gt2b[:], ALU.mult, [B_wfo[kk // 11], B_gt2b], [B_wfo[kk // 11]])

                W = GF * 128
                NGF = NT // GF
                xt = Ring([(sbt(ph, f"fxt{i}", [128, D], F32), Buf()) for i in range(2)])
                xs = Ring([(sbt(ph, f"fxs{i}", [128, D], BF16), Buf()) for i in range(2)])
                hTs = [sbt(ph, f"fhT{i}", [128, KC, W], BF16) for i in range(2)]
                B_hTs = [[Buf() for _ in range(GF)] for _ in range(2)]
                hid = sbt(ph, "hid", [128, NHC, W], BF16)
                B_hid = [Buf() for _ in range(NHC)]
                sl = Ring([(sbt(ph, f"sl{i}", [128, W], BF16), Buf()) for i in range(2)])
                xo = Ring([(sbt(ph, f"fxo{i}", [128, D], F32), Buf()) for i in range(2)])
                st2 = sbt(ph, "st2", [128, NT, 4], F32)
                B_st2 = [[Buf() for _ in range(3)] for _ in range(NT)]
                adab = [sbt(ph, f"adab_{i}", [128, KC, MH], BF16) for i in range(2)]
                browf = [sbt(ph, f"brow_{i}", [128, MH], F32) for i in range(2)]
                mrowf = [sbt(ph, f"mrow_{i}", [128, MH], F32) for i in range(2)]
                Bq = [[Buf(), Buf(), Buf()] for _ in range(2)]
                mod_next = list(range(NMB)) if l + 1 < L else []
                mod_loaded = []
                mod_done = []
                f_state = {}

                def prep_a(t):
                    xtt, B_xt = xt.next()
                    dma("sp", xtt[:], x_mid[t * 128:(t + 1) * 128, :], [B_mid[t]], [B_xt])
                    xst, B_xs = xs.next()
                    act(xst[:], xtt[:], AF.Square, [B_xt], [B_st2[t][0], B_xs], accum=st2[:, t, 0:1])
                    rstd(st2[:, t, 2:3], st2[:, t, 0:1], st2[:, t, 1:2], D, 1, B_st2[t][0], B_st2[t][1], B_st2[t][2])
                    ts("dve", xst[:], xtt[:], st2[:, t, 2:3], None, ALU.mult, None, [B_xt, B_st2[t][2]], [B_xs])
                    f_state[t] = (xst, B_xs)

                def prep_b(t, par, i):
                    xst, B_xs = f_state.pop(t)
                    pt, B_pt = ptring.next()
                    for kc in range(KC):
                        tr(pt[:, kc * 128:(kc + 1) * 128], xst[:, kc * 128:(kc + 1) * 128], [B_xs], [B_pt])
                    for kc in range(KC):
                        ts("dve", hTs[par][:, kc, i * 128:(i + 1) * 128], pt[:, kc * 128:(kc + 1) * 128],
                           gm2[:, kc:kc + 1], modT2[:, 3, kc:kc + 1], ALU.mult, ALU.add,
                           [B_pt, B_gm2, B_modT2], [B_hTs[par][i]])

                for i in range(GF):
                    prep_a(i)
                    prep_b(i, 0, i)
                a_at = {1: 0, 11: 1}
                b_at = {6: 0, 16: 1}
                for g in range(NGF):
                    t0 = g * GF
                    par = g % 2
                    hT = hTs[par]
                    B_hT = B_hTs[par]
                    loads = {}
                    for hcx in range(NHC):
                        pg, B_pg = pring.next()
                        c0 = hcx * 128
                        for kc in range(KC):
                            mm(pg[:, 0:W], wfi[:, kc, c0:c0 + 128], hT[:, kc, :], kc == 0, kc == KC - 1, B_hT + [B_wfi[c0 // 512]], [B_pg])
                        slt, B_sl = sl.next()
                        act(slt[:], pg[:, 0:W], AF.Silu, [B_pg], [B_sl])
                        pu, B_pu = pring.next()
                        c1 = DFF + hcx * 128
                        for kc in range(KC):
                            mm(pu[:, 0:W], wfi[:, kc, c1:c1 + 128], hT[:, kc, :], kc == 0, kc == KC - 1, B_hT + [B_wfi[c1 // 512]], [B_pu])
                        tt("dve", hid[:, hcx, :], pu[:, 0:W], slt[:], ALU.mult, [B_pu, B_sl], [B_hid[hcx]])
                        if g + 1 < NGF:
                            if hcx in a_at:
                                prep_a(t0 + GF + a_at[hcx])
                            if hcx in b_at:
                                prep_b(t0 + GF + b_at[hcx], 1 - par, b_at[hcx])
                        if hcx in (4, 14) and (mod_next or mod_loaded or mod_done) and g >= 1:
                            if mod_done:
                                j = mod_done.pop(0)
                                mod_tr(l + 1, j, mrowf[j % 2], Bq[j % 2][2])
                            if mod_loaded:
                                j = mod_loaded.pop(0)
                                r = j % 2
                                mod_compute(l + 1, j, adab[r], Bq[r][0], browf[r], Bq[r][1], mrowf[r], Bq[r][2])
                                mod_done.append(j)
                            if mod_next:
                                j = mod_next.pop(0)
                                r = j % 2
                                mod_load(l + 1, j, adab[r], Bq[r][0], browf[r], Bq[r][1])
                                mod_loaded.append(j)
                        if hcx == NHC - 2:
                            for i in range(GF):
                                t = t0 + i
                                xot, B_xo = xo.next()
                                dma("sp", xot[:], x_mid[t * 128:(t + 1) * 128, :], [B_mid[t]], [B_xo])
                                loads[i] = (xot, B_xo)
                    if g == 0:
                        setup_wfo()
                    for i in range(GF):
                        t = t0 + i
                        xot, B_xo = loads[i]
                        for hc in range(2):
                            po, B_po = pring.next()
                            for kk in range(NHC):
                                mm(po[:], hid[:, kk, i * 128:(i + 1) * 128], wfo[:, kk, hc * 512:(hc + 1) * 512], kk == 0, kk == NHC - 1,
                                   B_hid + B_wfo, [B_po])
                            tt("dve", xot[:, hc * 512:(hc + 1) * 512], po[:], xot[:, hc * 512:(hc + 1) * 512], ALU.add,
                               [B_po, B_xo], [B_xo])
                        dma("sp", x_dst[t * 128:(t + 1) * 128, :], xot[:], [B_xo], [B_dst[t]])
                while mod_next or mod_loaded or mod_done:
                    if mod_done:
                        j = mod_done.pop(0)
                        mod_tr(l + 1, j, mrowf[j % 2], Bq[j % 2][2])
                    if mod_loaded:
                        j = mod_loaded.pop(0)
                        r = j % 2
                        mod_compute(l + 1, j, adab[r], Bq[r][0], browf[r], Bq[r][1], mrowf[r], Bq[r][2])
                        mod_done.append(j)
                    if mod_next:
                        j = mod_next.pop(0)
                        r = j % 2
                        mod_load(l + 1, j, adab[r], Bq[r][0], browf[r], Bq[r][1])
                        mod_loaded.append(j)
                S.barrier()
        S.emit(nc, top)
    return nc


def alibi_bias_table():
    s = np.arange(128)[:, None]
    t = np.arange(128)[None, :]
    tab = np.zeros((128, 3, 2, 4, 128), np.float32)
    for ri, rel in enumerate((-1, 0, 1)):
        dist = np.abs(t - s - 128 * rel).astype(np.float32)
        for hh in range(2):
            for kvjj in range(4):
                kv, jj = divmod(kvjj, 2)
                h = kv * 4 + jj * 2 + hh
                slope = 2.0 ** (-8.0 * (h + 1) / 8.0)
                b = -slope * dist
                b = np.where(dist <= 128, b, NEG)
                tab[:, ri, hh, kvjj, :] = b
    return tab.reshape(128, 3, 2, 512)


_PROG = {}
WEIGHT_KEYS = ["w_ada", "b_ada", "norm1_g", "w_in", "gm_v_g", "gm_w_s", "gm_b_s", "q_norm_g", "k_norm_g",
               "attn_sink", "w_a", "w_b", "w_o", "norm2_g", "w_ffn_in", "w_ffn_out"]
N_LAYERS_PER_LAUNCH = 4


def kernel(**inputs):
    x = np.ascontiguousarray(inputs["x"], dtype=np.float32)
    c = np.ascontiguousarray(inputs["c"], dtype=np.float32)
    Bsz, T, _ = x.shape
    depth = inputs["w_ada"].shape[0]
    Lp = N_LAYERS_PER_LAUNCH
    key = (T, Lp)
    if key not in _PROG:
        _PROG[key] = build_program(T, Lp)
    nc = _PROG[key]
    abias = alibi_bias_table()
    cur = [x[b] for b in range(Bsz)]
    for l0 in range(0, depth, Lp):
        w = {k: np.ascontiguousarray(inputs[k][l0:l0 + Lp], dtype=np.float32) for k in WEIGHT_KEYS}
        in_maps = []
        for b in range(Bsz):
            m = {"x": cur[b], "c": c[b:b + 1], "abias": abias}
            m.update(w)
            in_maps.append(m)
        res = run_bass_kernel_spmd(nc, in_maps, core_ids=list(range(Bsz)))
        cur = [np.asarray(res.results[b]["out"], dtype=np.float32) for b in range(Bsz)]
    return np.stack(cur, axis=0)
```

```python
import numpy as np
from contextlib import ExitStack
import concourse.bass as bass
import concourse.mybir as mybir
from concourse.bass_utils import run_bass_kernel_spmd

F32 = mybir.dt.float32
BF16 = mybir.dt.bfloat16
AF = mybir.ActivationFunctionType
ALU = mybir.AluOpType
AX = mybir.AxisListType

D = 1024
KC = 8
DFF = 2816
NHC = 22
INW = 3840
U0, V0, Q0, K0, VA0, GA0, GB0 = 0, 512, 1024, 1536, 1664, 1792, 2816
EPS = 1e-6
NEG = -30000.0

ENGS = ("pe", "act", "dve", "pool", "sp")
SEM_CHUNK = 4000
DMA_POOL = 12


class Buf:
    __slots__ = ("name", "w", "r", "rd")

    def __init__(self, name=""):
        self.name = name
        self.w = None
        self.r = {}
        self.rd = []


class Op:
    __slots__ = ("eng", "idx", "fn", "deps", "signal", "ndma", "dsem", "dval", "sigidx", "name")


class Sched:
    def __init__(self):
        self.ops = {e: [] for e in ENGS}
        self.dma_count = {e: 0 for e in ENGS}
        self.dma_last = {}
        self.dma_cum = {}

    def op(self, eng, fn, reads=(), writes=(), ndma=0, name=None):
        o = Op()
        o.eng = eng
        o.fn = fn
        o.ndma = ndma
        o.signal = False
        o.name = name
        o.idx = len(self.ops[eng])
        o.dsem = None
        o.dval = 0
        o.sigidx = -1
        deps = []
        raw = set()
        for b in reads:
            if b.w is not None:
                deps.append(b.w)
                raw.add(id(b.w))
        for b in writes:
            if b.w is not None:
                deps.append(b.w)
            for d_ in b.r.values():
                deps.append(d_)
                raw.add(id(d_))
            deps.extend(b.rd)
        if ndma:
            slot = self.dma_count[eng] % DMA_POOL
            self.dma_count[eng] += 1
            key = (eng, slot)
            prev = self.dma_last.get(key)
            if prev is not None:
                deps.append(prev)
            self.dma_last[key] = o
            cum = self.dma_cum.get(key, 0) + 16 * ndma
            self.dma_cum[key] = cum
            o.dsem = key
            o.dval = cum
        out = []
        seen = set()
        for d in deps:
            if d is o or id(d) in seen:
                continue
            seen.add(id(d))
            if d.ndma == 0 and d.eng == eng and ndma == 0:
                if eng == "pe":
                    continue
                if id(d) not in raw:
                    continue
            out.append(d)
        o.deps = out
        for d in out:
            if d.ndma == 0:
                d.signal = True
        for b in reads:
            if ndma:
                b.rd.append(o)
            else:
                b.r[eng] = o
        for b in writes:
            b.w = o
            b.r = {}
            b.rd = []
        self.ops[eng].append(o)
        return o

    def barrier(self):
        lasts = [self.ops[e][-1] for e in ENGS if self.ops[e]]
        pend = list(self.dma_last.values())
        for e in ENGS:
            o = self.op(e, lambda eng: eng.nop(), name="barrier")
            for d in lasts + pend:
                if d is o or (d.ndma == 0 and d.eng == e):
                    continue
                if d not in o.deps:
                    o.deps.append(d)
                    if d.ndma == 0:
                        d.signal = True

    def emit(self, nc, stack):
        nsem = {}
        for e in ENGS:
            k = 0
            for o in self.ops[e]:
                if o.ndma == 0 and o.signal:
                    o.sigidx = k
                    k += 1
            nsem[e] = (k + SEM_CHUNK - 1) // SEM_CHUNK
        esems = {e: [stack.enter_context(nc.semaphore(f"s_{e}_{i}")) for i in range(nsem[e])] for e in ENGS}
        dsems = {key: stack.enter_context(nc.semaphore(f"d_{key[0]}_{key[1]}")) for key in self.dma_cum}
        block = stack.enter_context(nc.Block())

        def run(ename, eng):
            seen = {}
            for o in self.ops[ename]:
                for d in o.deps:
                    if d.ndma:
                        sem = dsems[d.dsem]
                        val = d.dval
                        key = ("d",) + d.dsem
                    else:
                        sem = esems[d.eng][d.sigidx // SEM_CHUNK]
                        val = d.sigidx % SEM_CHUNK + 1
                        key = ("e", d.eng, d.sigidx // SEM_CHUNK)
                    if seen.get(key, 0) >= val:
                        continue
                    seen[key] = val
                    eng.wait_ge(sem, val)
                r = o.fn(eng)
                if o.ndma:
                    rs = r if isinstance(r, (list, tuple)) else [r]
                    assert len(rs) == o.ndma, (o.name, len(rs), o.ndma)
                    for ins in rs:
                        ins.then_inc(dsems[o.dsem], 16)
                elif o.signal:
                    ins = r[-1] if isinstance(r, (list, tuple)) else r
                    ins.then_inc(esems[ename][o.sigidx // SEM_CHUNK], 1)

        @block.tensor
        def _(eng):
            run("pe", eng)

        @block.scalar
        def _(eng):
            run("act", eng)

        @block.vector
        def _(eng):
            run("dve", eng)

        @block.gpsimd
        def _(eng):
            run("pool", eng)

        @block.sync
        def _(eng):
            run("sp", eng)


class Ring:
    def __init__(self, items):
        self.items = items
        self.i = 0

    def next(self):
        it = self.items[self.i % len(self.items)]
        self.i += 1
        return it


def build_program(T, L):
    NT = T // 128
    GM = 2
    GF = 2
    nc = bass.Bass("TRN2", target_bir_lowering=False)

    def din(name, shape):
        return nc.dram_tensor(name, shape, F32, kind="ExternalInput").ap()

    x_d = din("x", [T, D])
    c_d = din("c", [1, D])
    w_ada_d = din("w_ada", [L, D, 6 * D])
    b_ada_d = din("b_ada", [L, 6 * D])
    norm1_d = din("norm1_g", [L, D])
    w_in_d = din("w_in", [L, D, INW])
    vg_d = din("gm_v_g", [L, 512])
    ws_d = din("gm_w_s", [L, 8, 128, 128])
    bs_d = din("gm_b_s", [L, 8, 128])
    qg_d = din("q_norm_g", [L, 64])
    kg_d = din("k_norm_g", [L, 64])
    sink_d = din("attn_sink", [L, 8])
    wa_d = din("w_a", [L, 512, D])
    wb_d = din("w_b", [L, 512, D])
    wo_d = din("w_o", [L, D, D])
    norm2_d = din("norm2_g", [L, D])
    wfi_d = din("w_ffn_in", [L, D, 2 * DFF])
    wfo_d = din("w_ffn_out", [L, DFF, D])
    abias_d = din("abias", [128, 3, 2, 512])
    out_d = nc.dram_tensor("out", [T, D], F32, kind="ExternalOutput").ap()
    xa_d = nc.dram_tensor("xa_scratch", [T, D], F32).ap()
    xb_d = nc.dram_tensor("xb_scratch", [T, D], F32).ap()
    mod_d = nc.dram_tensor("mod_scratch", [L, 6 * D], F32).ap()
    wfi_bf_d = nc.dram_tensor("wfi_bf_scratch", [L, D, 2 * DFF], BF16).ap()
    wfo_bf_d = nc.dram_tensor("wfo_bf_scratch", [L, DFF, D], BF16).ap()

    S = Sched()
    dram_x = {k: [Buf(k) for _ in range(NT)] for k in ("x", "xa", "xb", "out")}

    with ExitStack() as top:
        _cnt = [0]

        def sbt(st, name, shape, dt):
            _cnt[0] += 1
            return st.enter_context(nc.sbuf_tensor(f"{name}_{_cnt[0]}", shape, dt))

        pbanks = []
        for i in range(6):
            pbanks.append((top.enter_context(nc.psum_tensor(f"pb{i}", [128, 512], F32)), Buf(f"pb{i}")))
        pring = Ring(pbanks)
        ptbanks = []
        for i in range(2):
            ptbanks.append((top.enter_context(nc.psum_tensor(f"pt{i}", [128, 1024], BF16)), Buf(f"pt{i}")))
        ptring = Ring(ptbanks)

        ident = sbt(top, "ident", [128, 128], BF16)
        scT = sbt(top, "scT", [128, KC], BF16)
        cT = sbt(top, "cT", [128, KC], F32)
        Etab = sbt(top, "Etab", [128, 3, 2, 512], BF16)
        cneg = sbt(top, "cneg", [128, 8], F32)
        modT_all = sbt(top, "modT_all", [128, L, 6, KC], F32)
        g1T_all = sbt(top, "g1T_all", [128, L, KC], F32)
        g2T_all = sbt(top, "g2T_all", [128, L, KC], F32)
        one_f = sbt(top, "one_f", [1, 8], F32)
        scB = sbt(top, "scB", [128, KC, 128], BF16)
        identF = sbt(top, "identF", [128, 128], F32)
        dscr = sbt(top, "dscr", [128, 128], F32)
        B_scB, B_identF, B_dscr = Buf(), Buf(), Buf()
        B_modTa = [Buf() for _ in range(L)]
        B_gall, B_onef = Buf(), Buf()
        B_ident, B_scT, B_cT, B_E, B_cneg = Buf(), Buf(), Buf(), Buf(), Buf()

        S.op("pool", lambda e: e.memset(ident[:], 0.0), writes=[B_ident])
        S.op("pool", lambda e: e.affine_select(out=ident[:], in_=ident[:], pattern=[[-1, 128]],
                                               compare_op=ALU.not_equal, fill=1.0, base=0,
                                               channel_multiplier=1), reads=[B_ident], writes=[B_ident])
        S.op("pool", lambda e: e.memset(cneg[:], -0.5), writes=[B_cneg])
        S.op("pool", lambda e: e.memset(one_f[:], 1.0), writes=[B_onef])
        for l_ in range(L):
            S.op("act", (lambda a: lambda e: e.dma_start(out=g1T_all[:, a, :], in_=norm1_d[a, :].rearrange("(kc p) -> p kc", p=128),
                                                         allow_slow_non_contiguous=True))(l_), writes=[B_gall], ndma=1)
            S.op("act", (lambda a: lambda e: e.dma_start(out=g2T_all[:, a, :], in_=norm2_d[a, :].rearrange("(kc p) -> p kc", p=128),
                                                         allow_slow_non_contiguous=True))(l_), writes=[B_gall], ndma=1)
        S.op("sp", lambda e: e.dma_start(out=cT[:], in_=c_d[0, :].rearrange("(kc p) -> p kc", p=128),
                                         allow_slow_non_contiguous=True), writes=[B_cT], ndma=1)
        S.op("act", lambda e: e.activation(out=scT[:], in_=cT[:], func=AF.Silu), reads=[B_cT], writes=[B_scT])
        S.op("dve", lambda e: e.tensor_copy(out=scB[:], in_=scT[:].unsqueeze(2).to_broadcast([128, KC, 128])), reads=[B_scT], writes=[B_scB])
        S.op("dve", lambda e: e.tensor_copy(out=identF[:], in_=ident[:]), reads=[B_ident], writes=[B_identF])
        def dma(eng, out_ap, in_ap, reads, writes, slow=False):
            if slow:
                return S.op(eng, lambda e: e.dma_start(out=out_ap, in_=in_ap, allow_slow_non_contiguous=True),
                            reads=reads, writes=writes, ndma=1)
            return S.op(eng, lambda e: e.dma_start(out=out_ap, in_=in_ap), reads=reads, writes=writes, ndma=1)

        def mm(out_ap, lhsT, rhs, start, stop, reads, writes):
            return S.op("pe", lambda e: e.matmul(out_ap, lhsT=lhsT, rhs=rhs, start=start, stop=stop),
                        reads=reads, writes=writes)

        def tr(out_ap, in_ap, reads, writes):
            return S.op("pe", lambda e: e.transpose(out_ap, in_ap, ident[:]), reads=list(reads) + [B_ident], writes=writes)

        def act(out_ap, in_ap, func, reads, writes, scale=None, bias=None, accum=None):
            def f(e):
                kw = {}
                if scale is not None:
                    kw["scale"] = scale
                if bias is not None:
                    kw["bias"] = bias
                if accum is not None:
                    kw["accum_out"] = accum
                return e.activation(out=out_ap, in_=in_ap, func=func, **kw)
            return S.op("act", f, reads=reads, writes=writes)

        def tt(eng, out_ap, in0, in1, op, reads, writes):
            return S.op(eng, lambda e: e.tensor_tensor(out=out_ap, in0=in0, in1=in1, op=op), reads=reads, writes=writes)

        def ts(eng, out_ap, in0, s1, s2, op0, op1, reads, writes):
            if s2 is None:
                return S.op(eng, lambda e: e.tensor_scalar(out=out_ap, in0=in0, scalar1=s1, scalar2=None, op0=op0),
                            reads=reads, writes=writes)
            return S.op(eng, lambda e: e.tensor_scalar(out=out_ap, in0=in0, scalar1=s1, scalar2=s2, op0=op0, op1=op1),
                        reads=reads, writes=writes)

        def stt(eng, out_ap, in0, scalar, in1, op0, op1, reads, writes):
            return S.op(eng, lambda e: e.scalar_tensor_tensor(out=out_ap, in0=in0, scalar=scalar, in1=in1, op0=op0, op1=op1),
                        reads=reads, writes=writes)

        def cp(eng, out_ap, in_ap, reads, writes):
            return S.op(eng, lambda e: e.tensor_copy(out=out_ap, in_=in_ap), reads=reads, writes=writes)

        def rstd(out_ap, ss_ap, tmp_ap, n, w, B_ss, B_tmp, B_out):
            ts("dve", tmp_ap, ss_ap, 1.0 / n, EPS, ALU.mult, ALU.add, [B_ss], [B_tmp])
            tt("pool", out_ap, tmp_ap, cneg[:, 0:w], ALU.pow, [B_tmp, B_cneg], [B_out])

        B_modrow = [Buf() for _ in range(L)]
        B_wfi_bf = [[Buf() for _ in range(11)] for _ in range(L)]
        B_wfo_bf = [[Buf() for _ in range(2)] for _ in range(L)]
        MH = 256

        def mod_load(lay, j, adab_t, B_adab_t, brow_t, B_brow_t):
            dma("pool", adab_t[:], w_ada_d[lay, :, j * MH:(j + 1) * MH].rearrange("(kc p) n -> p kc n", p=128), [], [B_adab_t])
            dma("act", brow_t[:], b_ada_d[lay, j * MH:(j + 1) * MH].partition_broadcast(128), [], [B_brow_t])

        def mod_compute(lay, j, adab_t, B_adab_t, brow_t, B_brow_t, mrow_t, B_mrow_t):
            pb, B_pb = pring.next()
            for kc in range(KC):
                mm(pb[:, 0:MH], scB[:, kc, :], adab_t[:, kc, :], kc == 0, kc == KC - 1, [B_scB, B_adab_t], [B_pb])
            tt("dve", mrow_t[:], pb[:, 0:MH], brow_t[:], ALU.add, [B_pb, B_brow_t], [B_mrow_t])
            dma("sp", mod_d[lay:lay + 1, j * MH:(j + 1) * MH], mrow_t[0:1, :], [B_mrow_t], [B_modrow[lay]])

        def mod_tr(lay, j, mrow_t, B_mrow_t):
            nseg = MH // 128
            col = j * MH
            m_, kc_ = col // D, (col % D) // 128
            for sg in range(nseg):
                tt("dve", dscr[:], mrow_t[:, sg * 128:(sg + 1) * 128], identF[:], ALU.mult, [B_mrow_t, B_identF], [B_dscr])
                S.op("dve", (lambda o_, i_: lambda e: e.tensor_reduce(out=o_, in_=i_, axis=AX.X, op=ALU.add))(
                    modT_all[:, lay, m_, kc_ + sg:kc_ + sg + 1], dscr[:]), reads=[B_dscr], writes=[B_modTa[lay]])

        def mod_block(lay, j, adab_t, B_adab_t, brow_t, B_brow_t, mrow_t, B_mrow_t):
            mod_load(lay, j, adab_t, B_adab_t, brow_t, B_brow_t)
            mod_compute(lay, j, adab_t, B_adab_t, brow_t, B_brow_t, mrow_t, B_mrow_t)

        NMB = 6 * D // MH
        with ExitStack() as st1_:
            estage = sbt(st1_, "estage", [128, 3, 2, 512], F32)
            B_es = Buf()
            S.op("sp", lambda e: e.dma_start(out=estage[:], in_=abias_d), writes=[B_es], ndma=1)
            for ri_ in range(3):
                for hh_ in range(2):
                    S.op("act", (lambda a, b: lambda e: e.activation(out=Etab[:, a, b, :], in_=estage[:, a, b, :], func=AF.Exp))(ri_, hh_),
                         reads=[B_es], writes=[B_E])
            adab0 = [sbt(st1_, f"adab0_{i}", [128, KC, MH], BF16) for i in range(2)]
            brow0 = [sbt(st1_, f"brow0_{i}", [128, MH], F32) for i in range(2)]
            mrow0 = [sbt(st1_, f"mrow0_{i}", [128, MH], F32) for i in range(2)]
            Bq = [[Buf(), Buf(), Buf()] for _ in range(2)]
            mod_load(0, 0, adab0[0], Bq[0][0], brow0[0], Bq[0][1])
            for j in range(NMB):
                r = j % 2
                if j + 1 < NMB:
                    mod_load(0, j + 1, adab0[1 - r], Bq[1 - r][0], brow0[1 - r], Bq[1 - r][1])
                mod_compute(0, j, adab0[r], Bq[r][0], brow0[r], Bq[r][1], mrow0[r], Bq[r][2])
                if j >= 1:
                    mod_tr(0, j - 1, mrow0[1 - r], Bq[1 - r][2])
            mod_tr(0, NMB - 1, mrow0[(NMB - 1) % 2], Bq[(NMB - 1) % 2][2])
            S.barrier()

        for l in range(L):
            x_src = x_d if l == 0 else xb_d
            B_src = dram_x["x"] if l == 0 else dram_x["xb"]
            x_mid, B_mid = xa_d, dram_x["xa"]
            if l == L - 1:
                x_dst, B_dst = out_d, dram_x["out"]
            else:
                x_dst, B_dst = xb_d, dram_x["xb"]

            with ExitStack() as ph:
                win = sbt(ph, "win", [128, KC, INW], BF16)
                wa = sbt(ph, "wa", [128, 4, D], BF16)
                wb = sbt(ph, "wb", [128, 4, D], BF16)
                wo = sbt(ph, "wo", [128, KC, D], BF16)
                wsT = sbt(ph, "wsT", [128, 8, 128], BF16)
                modT = sbt(ph, "modT", [128, 6, KC], F32)
                g1T = sbt(ph, "g1T", [128, KC], F32)
                gm1 = sbt(ph, "gm1", [128, KC], F32)
                vgain = sbt(ph, "vgain", [128, 512], F32)
                gq = sbt(ph, "gq", [128, 64], F32)
                gk = sbt(ph, "gk", [128, 64], F32)
                gqk = sbt(ph, "gqk", [128, 64], F32)
                bsT = sbt(ph, "bsT", [128, 4, 128], F32)
                sinkE = sbt(ph, "sinkE", [128, 8], F32)
                B_win = [Buf() for _ in range(8)]
                B_wa, B_wb, B_wo, B_wsT, B_modT = Buf(), Buf(), Buf(), Buf(), Buf()
                B_g1T, B_gm1, B_vgain, B_gq, B_gk, B_gqk, B_bsT, B_sinkE, B_gt1b = (Buf() for _ in range(9))

                wsn = sbt(ph, "wsn", [128, 8, 128], BF16)
                gt1b = sbt(ph, "gt1b", [128, D], F32)
                B_wsn = Buf()
                cp("dve", modT[:], modT_all[:, l, :, :], [B_modTa[l]], [B_modT])
                stt("dve", gm1[:], modT[:, 1, :], 1.0, g1T_all[:, l, :], ALU.add, ALU.mult, [B_modT, B_gall], [B_gm1])
                dma("sp", gq[:], qg_d[l, :].partition_broadcast(128), [], [B_gq])
                dma("sp", gk[:], kg_d[l, :].partition_broadcast(128), [], [B_gk])
                stt("dve", gqk[:], gq[:], 0.125, gk[:], ALU.mult, ALU.mult, [B_gq, B_gk], [B_gqk])
                dma("sp", vgain[:], vg_d[l, :].partition_broadcast(128), [], [B_vgain])
                dma("sp", sinkE[:], sink_d[l, :].partition_broadcast(128), [], [B_sinkE])
                act(sinkE[:], sinkE[:], AF.Exp, [B_sinkE], [B_sinkE])
                for hh in range(2):
                    dma("sp", bsT[hh * 64:(hh + 1) * 64, :, :], bs_d[l, hh::2, :].unsqueeze(0).to_broadcast([64, 4, 128]), [], [B_bsT])
                dma("sp", gt1b[:], mod_d[l, 2 * D:3 * D].partition_broadcast(128), [B_modrow[l]], [B_gt1b])
                for j in (3, 1, 2, 0, 4, 5, 6, 7):
                    c0 = j * 512
                    c1 = min(INW, c0 + 512)
                    dma("pool", win[:, :, c0:c1], w_in_d[l, :, c0:c1].rearrange("(kc p) n -> p kc n", p=128), [], [B_win[j]])
                dma("pool", wsn[:], ws_d[l].rearrange("g t s -> t g s"), [], [B_wsn])
                dma("pool", wa[:], wa_d[l].rearrange("(kc p) n -> p kc n", p=128), [], [B_wa])
                dma("pool", wb[:], wb_d[l].rearrange("(kc p) n -> p kc n", p=128), [], [B_wb])
                dma("pool", wo[:], wo_d[l].rearrange("(kc p) n -> p kc n", p=128), [], [B_wo])

                def preconvert_ffn(part):
                    if part < 11:
                        j = part
                        dma("pool", wfi_bf_d[l, :, j * 512:(j + 1) * 512], wfi_d[l, :, j * 512:(j + 1) * 512], [], [B_wfi_bf[l][j]])
                    else:
                        hf = part - 11
                        dma("pool", wfo_bf_d[l, hf * 1408:(hf + 1) * 1408, :], wfo_d[l, hf * 1408:(hf + 1) * 1408, :], [], [B_wfo_bf[l][hf]])

                def setup_wsT():
                    pt, B_pt = ptring.next()
                    for g_ in range(8):
                        tr(pt[:, g_ * 128:(g_ + 1) * 128], wsn[:, g_, :], [B_wsn], [B_pt])
                    cp("dve", wsT[:], pt[:, 0:1024].rearrange("p (g t) -> p g t", t=128), [B_pt], [B_wsT])

                def setup_wo():
                    for kc in range(KC):
                        stt("dve", wo[:, kc, :], wo[:, kc, :], 0.5, gt1b[:], ALU.mult, ALU.mult, [B_wo, B_gt1b], [B_wo])

                NHT = 4
                W = GM * 128
                xt = Ring([(sbt(ph, f"xt{i}", [128, D], F32), Buf()) for i in range(2)])
                xs = Ring([(sbt(ph, f"xs{i}", [128, D], BF16), Buf()) for i in range(2)])
                hT = sbt(ph, "hT", [128, KC, NHT * 128], BF16)
                B_hT = [Buf() for _ in range(NHT)]
                uT = sbt(ph, "uT", [128, 4, W], BF16)
                B_uT = Buf()
                sgT = sbt(ph, "sgT", [128, 16, W], BF16)
                B_sgT = [Buf() for _ in range(16)]
                gv = Ring([(sbt(ph, f"gv{i}", [128, 512], BF16), Buf()) for i in range(2)])
                vn = Ring([(sbt(ph, f"vn{i}", [128, 512], BF16), Buf()) for i in range(2)])
                sq = Ring([(sbt(ph, f"sq{i}", [128, 512], F32), Buf()) for i in range(2)])
                qn = Ring([(sbt(ph, f"qn{i}", [128, 512], BF16), Buf()) for i in range(2)])
                ksq = Ring([(sbt(ph, f"ksq{i}", [128, 128], F32), Buf()) for i in range(2)])
                ktmp = Ring([(sbt(ph, f"ktmp{i}", [128, 128], F32), Buf()) for i in range(2)])
                kdup = Ring([(sbt(ph, f"kdup{i}", [128, 256], BF16), Buf()) for i in range(2)])
                qT = sbt(ph, "qT", [128, 4, W], BF16)
                B_qT = [Buf() for _ in range(GM)]
                kT = sbt(ph, "kT", [128, 2, 8 * 128], BF16)
                B_kT = [Buf() for _ in range(8)]
                va = sbt(ph, "va", [128, 8, 2, 65], BF16)
                B_va = [Buf() for _ in range(8)]
                aT = sbt(ph, "aT", [128, 4, W], BF16)
                B_aT = [Buf() for _ in range(GM)]
                oT = sbt(ph, "oT", [128, 4, W], BF16)
                B_oT = [Buf() for _ in range(GM)]
                otok = Ring([(sbt(ph, f"otok{i}", [128, 512], BF16), Buf()) for i in range(2)])
                gtmp = Ring([(sbt(ph, f"gtmp{i}", [128, 512], BF16), Buf()) for i in range(2)])
                expP = Ring([(sbt(ph, f"expP{i}", [128, 512], BF16), Buf()) for i in range(2)])
                PTr = Ring([(sbt(ph, f"PT{i}", [128, 3, 2, 512], BF16), [[Buf() for _ in range(2)] for _ in range(3)]) for i in range(2)])
                tmpA = sbt(ph, "tmpA", [128, KC, W], BF16)
                B_tmpA = [Buf() for _ in range(KC)]
                tmpB = Ring([(sbt(ph, f"tmpB{i}", [128, W], BF16), Buf()) for i in range(2)])
                mT = sbt(ph, "mT", [128, KC, W], BF16)
                B_mT = [Buf() for _ in range(KC)]
                xo = Ring([(sbt(ph, f"xo{i}", [128, D], F32), Buf()) for i in range(2)])
                NSL = 8
                st1 = sbt(ph, "st1", [128, NSL, 8], F32)
                stq = sbt(ph, "stq", [128, NSL, 3, 8], F32)
                stk = sbt(ph, "stk", [128, NSL, 3, 2], F32)
                dn = Ring([(sbt(ph, f"dn{i}", [128, 8], F32), Buf()) for i in range(2)])
                B_st = [[Buf() for _ in range(12)] for _ in range(NT)]
                B_st = [B_st[t_ % 8] for t_ in range(NT)]
                S.op("pool", (lambda v_: lambda e: e.memset(v_[:, :, :, 64:65], 1.0))(va), writes=B_va)

                e_state = {}

                def e_load(t):
                    xtt, B_xt = xt.next()
                    dma("sp", xtt[:], x_src[t * 128:(t + 1) * 128, :], [B_src[t]], [B_xt])
                    e_state[("x", t)] = (xtt, B_xt)

                def e_step_a(t):
                    if ("x", t) not in e_state:
                        e_load(t)
                    xtt, B_xt = e_state.pop(("x", t))
                    xst, B_xs = xs.next()
                    act(xst[:], xtt[:], AF.Square, [B_xt], [B_st[t][0], B_xs], accum=st1[:, t % 8, 0:1])
                    rstd(st1[:, t % 8, 2:3], st1[:, t % 8, 0:1], st1[:, t % 8, 1:2], D, 1, B_st[t][0], B_st[t][1], B_st[t][2])
                    ts("dve", xst[:], xtt[:], st1[:, t % 8, 2:3], None, ALU.mult, None, [B_xt, B_st[t][2]], [B_xs])
                    e_state[t] = (xst, B_xs)

                def e1(t):
                    s4 = t % NHT
                    xst, B_xs = e_state.pop(t)
                    pt, B_pt = ptring.next()
                    for kc in range(KC):
                        tr(pt[:, kc * 128:(kc + 1) * 128], xst[:, kc * 128:(kc + 1) * 128], [B_xs], [B_pt])
                    for kc in range(KC):
                        ts("dve", hT[:, kc, s4 * 128:(s4 + 1) * 128], pt[:, kc * 128:(kc + 1) * 128],
                           gm1[:, kc:kc + 1], modT[:, 0, kc:kc + 1], ALU.mult, ALU.add,
                           [B_pt, B_gm1, B_modT], [B_hT[s4]])

                def e2(t):
                    s4 = t % NHT
                    s8 = t % 8
                    pb, B_pb = pring.next()
                    for kc in range(KC):
                        mm(pb[:, 0:256], hT[:, kc, s4 * 128:(s4 + 1) * 128], win[:, kc, K0:K0 + 256], kc == 0, kc == KC - 1,
                           [B_hT[s4], B_win[3]], [B_pb])
                    act(va[:, s8, :, 0:64], pb[:, 128:256].rearrange("p (kv d) -> p kv d", d=64), AF.Copy, [B_pb], [B_va[s8]])
                    ks, B_ks = ksq.next()
                    act(ks[:], pb[:, 0:128], AF.Square, [B_pb], [B_ks])
                    S.op("dve", (lambda o_, i_: lambda e: e.tensor_reduce(out=o_, in_=i_, axis=AX.X, op=ALU.add))(
                        stk[:, t % 8, 0, :], ks[:].rearrange("p (h d) -> p h d", d=64)), reads=[B_ks], writes=[B_st[t][6]])
                    rstd(stk[:, t % 8, 2, :], stk[:, t % 8, 0, :], stk[:, t % 8, 1, :], 64, 2, B_st[t][6], B_st[t][7], B_st[t][8])
                    kt_, B_kt = ktmp.next()
                    tt("dve", kt_[:].rearrange("p (h d) -> p h d", d=64), pb[:, 0:128].rearrange("p (h d) -> p h d", d=64),
                       stk[:, t % 8, 2, :].unsqueeze(2).to_broadcast([128, 2, 64]), ALU.mult, [B_pb, B_st[t][8]], [B_kt])
                    kd, B_kd = kdup.next()
                    for dup in range(2):
                        tt("dve", kd[:].rearrange("p (kv u d) -> p kv u d", kv=2, u=2)[:, :, dup, :],
                           kt_[:].rearrange("p (h d) -> p h d", d=64),
                           gqk[:].unsqueeze(1).to_broadcast([128, 2, 64]), ALU.mult, [B_kt, B_gqk], [B_kd])
                    e_state[("k", t)] = (kd, B_kd)

                def e3(t):
                    s8 = t % 8
                    kd, B_kd = e_state.pop(("k", t))
                    pt2, B_pt2 = ptring.next()
                    for kv in range(2):
                        tr(pt2[:, kv * 128:(kv + 1) * 128], kd[:, kv * 128:(kv + 1) * 128], [B_kd], [B_pt2])
                    cp("dve", kT[:, :, s8 * 128:(s8 + 1) * 128], pt2[:, 0:256].rearrange("p (kv t) -> p kv t", t=128), [B_pt2], [B_kT[s8]])

                def tok_mm(t, i):
                    s4 = t % NHT
                    pb, B_pb = pring.next()
                    for kc in range(KC):
                        mm(pb[:], hT[:, kc, s4 * 128:(s4 + 1) * 128], win[:, kc, V0:V0 + 512], kc == 0, kc == KC - 1,
                           [B_hT[s4], B_win[1]], [B_pb])
                    gvt, B_gv = gv.next()
                    act(gvt[:], pb[:], AF.Gelu_apprx_tanh, [B_pb], [B_gv])
                    vnt, B_vn = vn.next()
                    act(vnt[:], gvt[:], AF.Square, [B_gv], [B_st[t][3], B_vn], accum=st1[:, t % 8, 3:4])
                    rstd(st1[:, t % 8, 5:6], st1[:, t % 8, 3:4], st1[:, t % 8, 4:5], 512, 1, B_st[t][3], B_st[t][4], B_st[t][5])
                    stt("dve", vnt[:], gvt[:], st1[:, t % 8, 5:6], vgain[:], ALU.mult, ALU.mult, [B_gv, B_st[t][5], B_vgain], [B_vn])
                    pq, B_pq = pring.next()
                    for kc in range(KC):
                        mm(pq[:], hT[:, kc, s4 * 128:(s4 + 1) * 128], win[:, kc, Q0:Q0 + 512], kc == 0, kc == KC - 1,
                           [B_hT[s4], B_win[2]], [B_pq])
                    sqt, B_sq = sq.next()
                    act(sqt[:], pq[:], AF.Square, [B_pq], [B_sq])
                    S.op("dve", (lambda o_, i_: lambda e: e.tensor_reduce(out=o_, in_=i_, axis=AX.X, op=ALU.add))(
                        stq[:, t % 8, 0, :], sqt[:].rearrange("p (h d) -> p h d", d=64)), reads=[B_sq], writes=[B_st[t][9]])
                    rstd(stq[:, t % 8, 2, :], stq[:, t % 8, 0, :], stq[:, t % 8, 1, :], 64, 8, B_st[t][9], B_st[t][10], B_st[t][11])
                    qnt, B_qn = qn.next()
                    tt("dve", qnt[:].rearrange("p (h d) -> p h d", d=64), pq[:].rearrange("p (h d) -> p h d", d=64),
                       stq[:, t % 8, 2, :].unsqueeze(2).to_broadcast([128, 8, 64]), ALU.mult, [B_pq, B_st[t][11]], [B_qn])
                    return (vnt, B_vn, qnt, B_qn)

                def q_tr(i, qnt, B_qn):
                    pt, B_pt = ptring.next()
                    for j in range(4):
                        tr(pt[:, j * 128:(j + 1) * 128], qnt[:, j * 128:(j + 1) * 128], [B_qn], [B_pt])
                    cp("dve", qT[:, :, i * 128:(i + 1) * 128], pt[:, 0:512].rearrange("p (j t) -> p j t", t=128), [B_pt], [B_qT[i]])

                def gmlp(i, vnt, B_vn):
                    pg, B_pg = pring.next()
                    for j in range(4):
                        for hh in range(2):
                            g = 2 * j + hh
                            mm(pg[hh * 64:(hh + 1) * 64, j * 128:(j + 1) * 128], vnt[:, g * 64:(g + 1) * 64], wsT[:, g, :],
                               True, True, [B_vn, B_wsT], [B_pg])
                    gt_, B_gt = gtmp.next()
                    tt("dve", gt_[:], pg[:], bsT[:].rearrange("p j t -> p (j t)"), ALU.add, [B_pg, B_bsT], [B_gt])
                    tt("pool", aT[:, :, i * 128:(i + 1) * 128], gt_[:].rearrange("p (j t) -> p j t", t=128),
                       uT[:, :, i * 128:(i + 1) * 128], ALU.mult, [B_gt, B_uT], [B_aT[i]])

                def attn_logits(t, i, filler=()):
                    rels = [r_ for r_ in (-1, 0, 1) if 0 <= t + r_ < NT]
                    PT, B_PT = PTr.next()
                    filler = list(filler)
                    for r_ in rels:
                        if r_ != rels[0] and filler:
                            c_wa(*filler.pop(0))
                        ri = r_ + 1
                        s8r = (t + r_) % 8
                        for hh in range(2):
                            pb, B_pb = pring.next()
                            for kv in range(2):
                                mm(pb[:, kv * 256:(kv + 1) * 256].rearrange("p (j t) -> p j t", t=128),
                                   kT[hh * 64:(hh + 1) * 64, kv, s8r * 128:(s8r + 1) * 128],
                                   qT[hh * 64:(hh + 1) * 64, kv * 2:kv * 2 + 2, i * 128:(i + 1) * 128],
                                   True, True, [B_kT[s8r], B_qT[i]], [B_pb])
                            ex, B_ex = expP.next()
                            act(ex[:], pb[:], AF.Exp, [B_pb], [B_ex])
                            tt("dve", PT[:, ri, hh, :], ex[:], Etab[:, ri, hh, :], ALU.mult, [B_ex, B_E], [B_PT[ri][hh]])
                    while filler:
                        c_wa(*filler.pop(0))
                    return (rels, PT, B_PT)

                def attn_pv(t, i, rels, PT, B_PT):
                    banks = [pring.next(), pring.next()]
                    for h in range(8):
                        j, hh = divmod(h, 2)
                        kv = j // 2
                        po, B_po = banks[h // 4]
                        c0 = (h % 4) * 65
                        for idx, r_ in enumerate(rels):
                            ri = r_ + 1
                            s8r = (t + r_) % 8
                            mm(po[:, c0:c0 + 65], PT[:, ri, hh, j * 128:(j + 1) * 128], va[:, s8r, kv, :],
                               idx == 0, idx == len(rels) - 1, [B_PT[ri][hh], B_va[s8r]], [B_po])
                    dnt, B_dn = dn.next()
                    for bi in range(2):
                        po, B_po = banks[bi]
                        tt("dve", dnt[:, bi * 4:(bi + 1) * 4].unsqueeze(2),
                           po[:, 0:260].rearrange("p (h e) -> p h e", e=65)[:, :, 64:65],
                           sinkE[:, bi * 4:(bi + 1) * 4].unsqueeze(2), ALU.add, [B_po, B_sinkE], [B_dn])
                    S.op("dve", lambda e: e.reciprocal(out=dnt[:], in_=dnt[:]), reads=[B_dn], writes=[B_dn])
                    ot, B_ot = otok.next()
                    for bi in range(2):
                        po, B_po = banks[bi]
                        tt("dve", ot[:, bi * 256:(bi + 1) * 256].rearrange("p (h d) -> p h d", d=64),
                           po[:, 0:260].rearrange("p (h e) -> p h e", e=65)[:, :, 0:64],
                           dnt[:, bi * 4:(bi + 1) * 4].unsqueeze(2).to_broadcast([128, 4, 64]), ALU.mult,
                           [B_po, B_dn], [B_ot])
                    return (ot, B_ot)

                def attn_ot(i, ot, B_ot):
                    pt, B_pt = ptring.next()
                    for j in range(4):
                        tr(pt[:, j * 128:(j + 1) * 128], ot[:, j * 128:(j + 1) * 128], [B_ot], [B_pt])
                    cp("dve", oT[:, :, i * 128:(i + 1) * 128], pt[:, 0:512].rearrange("p (j t) -> p j t", t=128), [B_pt], [B_oT[i]])

                def feat_chunk(g, kind, cc):
                    t0 = g * GM
                    s0 = (t0 % NHT) * 128
                    rb = [B_hT[(t0 + i) % NHT] for i in range(GM)]
                    c0 = (U0 if kind == "u" else GA0) + cc * 128
                    pb, B_pb = pring.next()
                    for kc in range(KC):
                        mm(pb[:, 0:W], win[:, kc, c0:c0 + 128], hT[:, kc, s0:s0 + W],
                           kc == 0, kc == KC - 1, rb + [B_win[c0 // 512]], [B_pb])
                    if kind == "u":
                        act(uT[:, cc, :], pb[:, 0:W], AF.Gelu_apprx_tanh, [B_pb], [B_uT])
                    else:
                        act(sgT[:, cc, :], pb[:, 0:W], AF.Tanh, [B_pb], [B_sgT[cc]], scale=0.5)

                def c_wa(d0, d1):
                    for dc in range(d0, d1):
                        pa, B_pa = pring.next()
                        for cc in range(4):
                            mm(pa[:, 0:W], wa[:, cc, dc * 128:(dc + 1) * 128], aT[:, cc, :], cc == 0, cc == 3, B_aT + [B_wa], [B_pa])
                        stt("dve", tmpA[:, dc, :], sgT[:, dc, :], 1.0, pa[:, 0:W], ALU.add, ALU.mult, [B_pa, B_sgT[dc]], [B_tmpA[dc]])

                def c_wb():
                    for dc in range(KC):
                        pb_, B_pb_ = pring.next()
                        for j in range(4):
                            mm(pb_[:, 0:W], wb[:, j, dc * 128:(dc + 1) * 128], oT[:, j, :], j == 0, j == 3, B_oT + [B_wb], [B_pb_])
                        tb, B_tb = tmpB.next()
                        stt("dve", tb[:], sgT[:, 8 + dc, :], 1.0, pb_[:, 0:W], ALU.add, ALU.mult, [B_pb_, B_sgT[8 + dc]], [B_tb])
                        tt("pool", mT[:, dc, :], tmpA[:, dc, :], tb[:], ALU.add, [B_tmpA[dc], B_tb], [B_mT[dc]])

                def c_wo(g):
                    t0 = g * GM
                    loads = []
                    for i in range(GM):
                        t = t0 + i
                        xot, B_xo = xo.next()
                        dma("sp", xot[:], x_src[t * 128:(t + 1) * 128, :], [B_src[t]], [B_xo])
                        loads.append((xot, B_xo))
                    for i in range(GM):
                        t = t0 + i
                        xot, B_xo = loads[i]
                        for hc in range(2):
                            po, B_po = pring.next()
                            for dc in range(KC):
                                mm(po[:], mT[:, dc, i * 128:(i + 1) * 128], wo[:, dc, hc * 512:(hc + 1) * 512], dc == 0, dc == KC - 1,
                                   B_mT + [B_wo], [B_po])
                            tt("dve", xot[:, hc * 512:(hc + 1) * 512], po[:], xot[:, hc * 512:(hc + 1) * 512], ALU.add,
                               [B_po, B_xo], [B_xo])
                        dma("sp", x_mid[t * 128:(t + 1) * 128, :], xot[:], [B_xo], [B_mid[t]])

                NG = NT // GM
                for t in range(GM):
                    e_step_a(t)
                    e1(t)
                    e2(t)
                    e3(t)
                toks = [tok_mm(i, i) for i in range(GM)]
                for cc in range(4):
                    feat_chunk(0, "u", cc)
                for g in range(NG):
                    t0 = g * GM
                    nxt = [t for t in (t0 + GM, t0 + GM + 1) if t < NT]
                    n0 = nxt[0] if len(nxt) > 0 else None
                    n1 = nxt[1] if len(nxt) > 1 else None
                    for t in nxt:
                        e_step_a(t)
                    for cc in range(8):
                        feat_chunk(g, "g", cc)
                    for i in range(GM):
                        q_tr(i, toks[i][2], toks[i][3])
                    if n0 is not None:
                        e1(n0)
                    for cc in range(8, 16):
                        feat_chunk(g, "g", cc)
                    if g == 0:
                        setup_wsT()
                    for i in range(GM):
                        gmlp(i, toks[i][0], toks[i][1])
                    if n0 is not None:
                        e2(n0)
                    if n1 is not None:
                        e1(n1)
                    lg0 = attn_logits(t0, 0, filler=[(0, 2), (2, 4)])
                    if n0 is not None:
                        e3(n0)
                    lg1 = attn_logits(t0 + 1, 1, filler=[(4, 6), (6, 8)])
                    if n1 is not None:
                        e2(n1)
                    o0 = attn_pv(t0, 0, *lg0)
                    o1 = attn_pv(t0 + 1, 1, *lg1)
                    for t_ in (t0 + 2 * GM, t0 + 2 * GM + 1):
                        if t_ < NT:
                            e_load(t_)
                    ntoks = []
                    if n0 is not None:
                        ntoks.append(tok_mm(n0, 0))
                    attn_ot(0, *o0)
                    attn_ot(1, *o1)
                    if n1 is not None:
                        ntoks.append(tok_mm(n1, 1))
                        e3(n1)
                    c_wb()
                    for part_ in range(13):
                        if (NG >= 15 and g == 1 + part_) or (NG < 15 and g == 0):
                            preconvert_ffn(part_)
                    if n0 is not None:
                        for cc in range(4):
                            feat_chunk(g + 1, "u", cc)
                    if g == 0:
                        setup_wo()
                    c_wo(g)
                    toks = ntoks
                S.barrier()

            with ExitStack() as ph:
                wfi = sbt(ph, "wfi", [128, KC, 2 * DFF], BF16)
                wfo = sbt(ph, "wfo", [128, NHC, D], BF16)
                modT2 = sbt(ph, "modT2", [128, 6, KC], F32)
                g2T = sbt(ph, "g2Tb", [128, KC], F32)
                gm2 = sbt(ph, "gm2", [128, KC], F32)
                gt2b = sbt(ph, "gt2b", [128, D], F32)
                B_wfi = [Buf() for _ in range(11)]
                B_wfo = [Buf() for _ in range(2)]
                B_modT2, B_g2T, B_gm2, B_gt2b = (Buf() for _ in range(4))
                cp("dve", modT2[:], modT_all[:, l, :, :], [B_modTa[l]], [B_modT2])
                stt("dve", gm2[:], modT2[:, 4, :], 1.0, g2T_all[:, l, :], ALU.add, ALU.mult, [B_modT2, B_gall], [B_gm2])
                dma("sp", gt2b[:], mod_d[l, 5 * D:6 * D].partition_broadcast(128), [B_modrow[l]], [B_gt2b])
                order = []
                for hcx in range(NHC):
                    for c_ in (hcx * 128, DFF + hcx * 128):
                        if c_ // 512 not in order:
                            order.append(c_ // 512)
                for j in order:
                    dma("pool", wfi[:, :, j * 512:(j + 1) * 512], wfi_bf_d[l, :, j * 512:(j + 1) * 512].rearrange("(kc p) n -> p kc n", p=128), [B_wfi_bf[l][j]], [B_wfi[j]])
                for hf in range(2):
                    dma("pool", wfo[:, hf * 11:(hf + 1) * 11, :], wfo_bf_d[l, hf * 11 * 128:(hf + 1) * 11 * 128, :].rearrange("(kc p) n -> p kc n", p=128), [B_wfo_bf[l][hf]], [B_wfo[hf]])

                def setup_wfo():
                    for kk in range(NHC):
                        tt("dve", wfo[:, kk, :], wfo[:, kk, :], gt2b[:], ALU.mult, [B_wfo[kk // 11], B_gt2b], [B_wfo[kk // 11]])

                W = GF * 128
                NGF = NT // GF
                xt = Ring([(sbt(ph, f"fxt{i}", [128, D], F32), Buf()) for i in range(2)])
                xs = Ring([(sbt(ph, f"fxs{i}", [128, D], BF16), Buf()) for i in range(2)])
                hTs = [sbt(ph, f"fhT{i}", [128, KC, W], BF16) for i in range(2)]
                B_hTs = [[Buf() for _ in range(GF)] for _ in range(2)]
                hid = sbt(ph, "hid", [128, NHC, W], BF16)
                B_hid = [Buf() for _ in range(NHC)]
                sl = Ring([(sbt(ph, f"sl{i}", [128, W], BF16), Buf()) for i in range(2)])
                xo = Ring([(sbt(ph, f"fxo{i}", [128, D], F32), Buf()) for i in range(2)])
                st2 = sbt(ph, "st2", [128, NT, 4], F32)
                B_st2 = [[Buf() for _ in range(3)] for _ in range(NT)]
                adab = [sbt(ph, f"adab_{i}", [128, KC, MH], BF16) for i in range(2)]
                browf = [sbt(ph, f"brow_{i}", [128, MH], F32) for i in range(2)]
                mrowf = [sbt(ph, f"mrow_{i}", [128, MH], F32) for i in range(2)]
                Bq = [[Buf(), Buf(), Buf()] for _ in range(2)]
                mod_next = list(range(NMB)) if l + 1 < L else []
                mod_loaded = []
                mod_done = []
                f_state = {}

                def prep_a(t):
                    xtt, B_xt = xt.next()
                    dma("sp", xtt[:], x_mid[t * 128:(t + 1) * 128, :], [B_mid[t]], [B_xt])
                    xst, B_xs = xs.next()
                    act(xst[:], xtt[:], AF.Square, [B_xt], [B_st2[t][0], B_xs], accum=st2[:, t, 0:1])
                    rstd(st2[:, t, 2:3], st2[:, t, 0:1], st2[:, t, 1:2], D, 1, B_st2[t][0], B_st2[t][1], B_st2[t][2])
                    ts("dve", xst[:], xtt[:], st2[:, t, 2:3], None, ALU.mult, None, [B_xt, B_st2[t][2]], [B_xs])
                    f_state[t] = (xst, B_xs)

                def prep_b(t, par, i):
                    xst, B_xs = f_state.pop(t)
                    pt, B_pt = ptring.next()
                    for kc in range(KC):
                        tr(pt[:, kc * 128:(kc + 1) * 128], xst[:, kc * 128:(kc + 1) * 128], [B_xs], [B_pt])
                    for kc in range(KC):
                        ts("dve", hTs[par][:, kc, i * 128:(i + 1) * 128], pt[:, kc * 128:(kc + 1) * 128],
                           gm2[:, kc:kc + 1], modT2[:, 3, kc:kc + 1], ALU.mult, ALU.add,
                           [B_pt, B_gm2, B_modT2], [B_hTs[par][i]])

                for i in range(GF):
                    prep_a(i)
                    prep_b(i, 0, i)
                a_at = {1: 0, 11: 1}
                b_at = {6: 0, 16: 1}
                for g in range(NGF):
                    t0 = g * GF
                    par = g % 2
                    hT = hTs[par]
                    B_hT = B_hTs[par]
                    loads = {}
                    for hcx in range(NHC):
                        pg, B_pg = pring.next()
                        c0 = hcx * 128
                        for kc in range(KC):
                            mm(pg[:, 0:W], wfi[:, kc, c0:c0 + 128], hT[:, kc, :], kc == 0, kc == KC - 1, B_hT + [B_wfi[c0 // 512]], [B_pg])
                        slt, B_sl = sl.next()
                        act(slt[:], pg[:, 0:W], AF.Silu, [B_pg], [B_sl])
                        pu, B_pu = pring.next()
                        c1 = DFF + hcx * 128
                        for kc in range(KC):
                            mm(pu[:, 0:W], wfi[:, kc, c1:c1 + 128], hT[:, kc, :], kc == 0, kc == KC - 1, B_hT + [B_wfi[c1 // 512]], [B_pu])
                        tt("dve", hid[:, hcx, :], pu[:, 0:W], slt[:], ALU.mult, [B_pu, B_sl], [B_hid[hcx]])
                        if g + 1 < NGF:
                            if hcx in a_at:
                                prep_a(t0 + GF + a_at[hcx])
                            if hcx in b_at:
                                prep_b(t0 + GF + b_at[hcx], 1 - par, b_at[hcx])
                        if hcx in (4, 14) and (mod_next or mod_loaded or mod_done) and g >= 1:
                            if mod_done:
                                j = mod_done.pop(0)
                                mod_tr(l + 1, j, mrowf[j % 2], Bq[j % 2][2])
                            if mod_loaded:
                                j = mod_loaded.pop(0)
                                r = j % 2
                                mod_compute(l + 1, j, adab[r], Bq[r][0], browf[r], Bq[r][1], mrowf[r], Bq[r][2])
                                mod_done.append(j)
                            if mod_next:
                                j = mod_next.pop(0)
                                r = j % 2
                                mod_load(l + 1, j, adab[r], Bq[r][0], browf[r], Bq[r][1])
                                mod_loaded.append(j)
                        if hcx == NHC - 2:
                            for i in range(GF):
                                t = t0 + i
                                xot, B_xo = xo.next()
                                dma("sp", xot[:], x_mid[t * 128:(t + 1) * 128, :], [B_mid[t]], [B_xo])
                                loads[i] = (xot, B_xo)
                    if g == 0:
                        setup_wfo()
                    for i in range(GF):
                        t = t0 + i
                        xot, B_xo = loads[i]
                        for hc in range(2):
                            po, B_po = pring.next()
                            for kk in range(NHC):
                                mm(po[:], hid[:, kk, i * 128:(i + 1) * 128], wfo[:, kk, hc * 512:(hc + 1) * 512], kk == 0, kk == NHC - 1,
                                   B_hid + B_wfo, [B_po])
                            tt("dve", xot[:, hc * 512:(hc + 1) * 512], po[:], xot[:, hc * 512:(hc + 1) * 512], ALU.add,
                               [B_po, B_xo], [B_xo])
                        dma("sp", x_dst[t * 128:(t + 1) * 128, :], xot[:], [B_xo], [B_dst[t]])
                while mod_next or mod_loaded or mod_done:
                    if mod_done:
                        j = mod_done.pop(0)
                        mod_tr(l + 1, j, mrowf[j % 2], Bq[j % 2][2])
                    if mod_loaded:
                        j = mod_loaded.pop(0)
                        r = j % 2
                        mod_compute(l + 1, j, adab[r], Bq[r][0], browf[r], Bq[r][1], mrowf[r], Bq[r][2])
                        mod_done.append(j)
                    if mod_next:
                        j = mod_next.pop(0)
                        r = j % 2
                        mod_load(l + 1, j, adab[r], Bq[r][0], browf[r], Bq[r][1])
                        mod_loaded.append(j)
                S.barrier()
        S.emit(nc, top)
    return nc


def alibi_bias_table():
    s = np.arange(128)[:, None]
    t = np.arange(128)[None, :]
    tab = np.zeros((128, 3, 2, 4, 128), np.float32)
    for ri, rel in enumerate((-1, 0, 1)):
        dist = np.abs(t - s - 128 * rel).astype(np.float32)
        for hh in range(2):
            for kvjj in range(4):
                kv, jj = divmod(kvjj, 2)
                h = kv * 4 + jj * 2 + hh
                slope = 2.0 ** (-8.0 * (h + 1) / 8.0)
                b = -slope * dist
                b = np.where(dist <= 128, b, NEG)
                tab[:, ri, hh, kvjj, :] = b
    return tab.reshape(128, 3, 2, 512)


_PROG = {}
WEIGHT_KEYS = ["w_ada", "b_ada", "norm1_g", "w_in", "gm_v_g", "gm_w_s", "gm_b_s", "q_norm_g", "k_norm_g",
               "attn_sink", "w_a", "w_b", "w_o", "norm2_g", "w_ffn_in", "w_ffn_out"]
N_LAYERS_PER_LAUNCH = 4


def kernel(**inputs):
    x = np.ascontiguousarray(inputs["x"], dtype=np.float32)
    c = np.ascontiguousarray(inputs["c"], dtype=np.float32)
    Bsz, T, _ = x.shape
    depth = inputs["w_ada"].shape[0]
    Lp = N_LAYERS_PER_LAUNCH
    key = (T, Lp)
    if key not in _PROG:
        _PROG[key] = build_program(T, Lp)
    nc = _PROG[key]
    abias = alibi_bias_table()
    cur = [x[b] for b in range(Bsz)]
    for l0 in range(0, depth, Lp):
        w = {k: np.ascontiguousarray(inputs[k][l0:l0 + Lp], dtype=np.float32) for k in WEIGHT_KEYS}
        in_maps = []
        for b in range(Bsz):
            m = {"x": cur[b], "c": c[b:b + 1], "abias": abias}
            m.update(w)
            in_maps.append(m)
        res = run_bass_kernel_spmd(nc, in_maps, core_ids=list(range(Bsz)))
        cur = [np.asarray(res.results[b]["out"], dtype=np.float32) for b in range(Bsz)]
    return np.stack(cur, axis=0)
```

```python
import numpy as np
from contextlib import ExitStack
import concourse.bass as bass
import concourse.mybir as mybir
from concourse.bass_utils import run_bass_kernel_spmd

F32 = mybir.dt.float32
BF16 = mybir.dt.bfloat16
AF = mybir.ActivationFunctionType
ALU = mybir.AluOpType
AX = mybir.AxisListType

D = 1024
KC = 8
DFF = 2816
NHC = 22
INW = 3840
U0, V0, Q0, K0, VA0, GA0, GB0 = 0, 512, 1024, 1536, 1664, 1792, 2816
EPS = 1e-6
NEG = -30000.0

ENGS = ("pe", "act", "dve", "pool", "sp")
SEM_CHUNK = 4000
DMA_POOL = 12


class Buf:
    __slots__ = ("name", "w", "r", "rd")

    def __init__(self, name=""):
        self.name = name
        self.w = None
        self.r = {}
        self.rd = []


class Op:
    __slots__ = ("eng", "idx", "fn", "deps", "signal", "ndma", "dsem", "dval", "sigidx", "name")


class Sched:
    def __init__(self):
        self.ops = {e: [] for e in ENGS}
        self.dma_count = {e: 0 for e in ENGS}
        self.dma_last = {}
        self.dma_cum = {}

    def op(self, eng, fn, reads=(), writes=(), ndma=0, name=None):
        o = Op()
        o.eng = eng
        o.fn = fn
        o.ndma = ndma
        o.signal = False
        o.name = name
        o.idx = len(self.ops[eng])
        o.dsem = None
        o.dval = 0
        o.sigidx = -1
        deps = []
        raw = set()
        for b in reads:
            if b.w is not None:
                deps.append(b.w)
                raw.add(id(b.w))
        for b in writes:
            if b.w is not None:
                deps.append(b.w)
            for d_ in b.r.values():
                deps.append(d_)
                raw.add(id(d_))
            deps.extend(b.rd)
        if ndma:
            slot = self.dma_count[eng] % DMA_POOL
            self.dma_count[eng] += 1
            key = (eng, slot)
            prev = self.dma_last.get(key)
            if prev is not None:
                deps.append(prev)
            self.dma_last[key] = o
            cum = self.dma_cum.get(key, 0) + 16 * ndma
            self.dma_cum[key] = cum
            o.dsem = key
            o.dval = cum
        out = []
        seen = set()
        for d in deps:
            if d is o or id(d) in seen:
                continue
            seen.add(id(d))
            if d.ndma == 0 and d.eng == eng and ndma == 0:
                if eng == "pe":
                    continue
                if id(d) not in raw:
                    continue
            out.append(d)
        o.deps = out
        for d in out:
            if d.ndma == 0:
                d.signal = True
        for b in reads:
            if ndma:
                b.rd.append(o)
            else:
                b.r[eng] = o
        for b in writes:
            b.w = o
            b.r = {}
            b.rd = []
        self.ops[eng].append(o)
        return o

    def barrier(self):
        lasts = [self.ops[e][-1] for e in ENGS if self.ops[e]]
        pend = list(self.dma_last.values())
        for e in ENGS:
            o = self.op(e, lambda eng: eng.nop(), name="barrier")
            for d in lasts + pend:
                if d is o or (d.ndma == 0 and d.eng == e):
                    continue
                if d not in o.deps:
                    o.deps.append(d)
                    if d.ndma == 0:
                        d.signal = True

    def emit(self, nc, stack):
        nsem = {}
        for e in ENGS:
            k = 0
            for o in self.ops[e]:
                if o.ndma == 0 and o.signal:
                    o.sigidx = k
                    k += 1
            nsem[e] = (k + SEM_CHUNK - 1) // SEM_CHUNK
        esems = {e: [stack.enter_context(nc.semaphore(f"s_{e}_{i}")) for i in range(nsem[e])] for e in ENGS}
        dsems = {key: stack.enter_context(nc.semaphore(f"d_{key[0]}_{key[1]}")) for key in self.dma_cum}
        block = stack.enter_context(nc.Block())

        def run(ename, eng):
            seen = {}
            for o in self.ops[ename]:
                for d in o.deps:
                    if d.ndma:
                        sem = dsems[d.dsem]
                        val = d.dval
                        key = ("d",) + d.dsem
                    else:
                        sem = esems[d.eng][d.sigidx // SEM_CHUNK]
                        val = d.sigidx % SEM_CHUNK + 1
                        key = ("e", d.eng, d.sigidx // SEM_CHUNK)
                    if seen.get(key, 0) >= val:
                        continue
                    seen[key] = val
                    eng.wait_ge(sem, val)
                r = o.fn(eng)
                if o.ndma:
                    rs = r if isinstance(r, (list, tuple)) else [r]
                    assert len(rs) == o.ndma, (o.name, len(rs), o.ndma)
                    for ins in rs:
                        ins.then_inc(dsems[o.dsem], 16)
                elif o.signal:
                    ins = r[-1] if isinstance(r, (list, tuple)) else r
                    ins.then_inc(esems[ename][o.sigidx // SEM_CHUNK], 1)

        @block.tensor
        def _(eng):
            run("pe", eng)

        @block.scalar
        def _(eng):
            run("act", eng)

        @block.vector
        def _(eng):
            run("dve", eng)

        @block.gpsimd
        def _(eng):
            run("pool", eng)

        @block.sync
        def _(eng):
            run("sp", eng)


class Ring:
    def __init__(self, items):
        self.items = items
        self.i = 0

    def next(self):
        it = self.items[self.i % len(self.items)]
        self.i += 1
        return it


def build_program(T, L):
    NT = T // 128
    GM = 2
    GF = 2
    nc = bass.Bass("TRN2", target_bir_lowering=False)

    def din(name, shape):
        return nc.dram_tensor(name, shape, F32, kind="ExternalInput").ap()

    x_d = din("x", [T, D])
    c_d = din("c", [1, D])
    w_ada_d = din("w_ada", [L, D, 6 * D])
    b_ada_d = din("b_ada", [L, 6 * D])
    norm1_d = din("norm1_g", [L, D])
    w_in_d = din("w_in", [L, D, INW])
    vg_d = din("gm_v_g", [L, 512])
    ws_d = din("gm_w_s", [L, 8, 128, 128])
    bs_d = din("gm_b_s", [L, 8, 128])
    qg_d = din("q_norm_g", [L, 64])
    kg_d = din("k_norm_g", [L, 64])
    sink_d = din("attn_sink", [L, 8])
    wa_d = din("w_a", [L, 512, D])
    wb_d = din("w_b", [L, 512, D])
    wo_d = din("w_o", [L, D, D])
    norm2_d = din("norm2_g", [L, D])
    wfi_d = din("w_ffn_in", [L, D, 2 * DFF])
    wfo_d = din("w_ffn_out", [L, DFF, D])
    abias_d = din("abias", [128, 3, 2, 512])
    out_d = nc.dram_tensor("out", [T, D], F32, kind="ExternalOutput").ap()
    xa_d = nc.dram_tensor("xa_scratch", [T, D], F32).ap()
    xb_d = nc.dram_tensor("xb_scratch", [T, D], F32).ap()
    mod_d = nc.dram_tensor("mod_scratch", [L, 6 * D], F32).ap()
    wfi_bf_d = nc.dram_tensor("wfi_bf_scratch", [L, D, 2 * DFF], BF16).ap()
    wfo_bf_d = nc.dram_tensor("wfo_bf_scratch", [L, DFF, D], BF16).ap()

    S = Sched()
    dram_x = {k: [Buf(k) for _ in range(NT)] for k in ("x", "xa", "xb", "out")}

    with ExitStack() as top:
        _cnt = [0]

        def sbt(st, name, shape, dt):
            _cnt[0] += 1
            return st.enter_context(nc.sbuf_tensor(f"{name}_{_cnt[0]}", shape, dt))

        pbanks = []
        for i in range(6):
            pbanks.append((top.enter_context(nc.psum_tensor(f"pb{i}", [128, 512], F32)), Buf(f"pb{i}")))
        pring = Ring(pbanks)
        ptbanks = []
        for i in range(2):
            ptbanks.append((top.enter_context(nc.psum_tensor(f"pt{i}", [128, 1024], BF16)), Buf(f"pt{i}")))
        ptring = Ring(ptbanks)

        ident = sbt(top, "ident", [128, 128], BF16)
        scT = sbt(top, "scT", [128, KC], BF16)
        cT = sbt(top, "cT", [128, KC], F32)
        Etab = sbt(top, "Etab", [128, 3, 2, 512], BF16)
        cneg = sbt(top, "cneg", [128, 8], F32)
        modT_all = sbt(top, "modT_all", [128, L, 6, KC], F32)
        g1T_all = sbt(top, "g1T_all", [128, L, KC], F32)
        g2T_all = sbt(top, "g2T_all", [128, L, KC], F32)
        one_f = sbt(top, "one_f", [1, 8], F32)
        scB = sbt(top, "scB", [128, KC, 128], BF16)
        identF = sbt(top, "identF", [128, 128], F32)
        dscr = sbt(top, "dscr", [128, 128], F32)
        B_scB, B_identF, B_dscr = Buf(), Buf(), Buf()
        B_modTa = [Buf() for _ in range(L)]
        B_gall, B_onef = Buf(), Buf()
        B_ident, B_scT, B_cT, B_E, B_cneg = Buf(), Buf(), Buf(), Buf(), Buf()

        S.op("pool", lambda e: e.memset(ident[:], 0.0), writes=[B_ident])
        S.op("pool", lambda e: e.affine_select(out=ident[:], in_=ident[:], pattern=[[-1, 128]],
                                               compare_op=ALU.not_equal, fill=1.0, base=0,
                                               channel_multiplier=1), reads=[B_ident], writes=[B_ident])
        S.op("pool", lambda e: e.memset(cneg[:], -0.5), writes=[B_cneg])
        S.op("pool", lambda e: e.memset(one_f[:], 1.0), writes=[B_onef])
        for l_ in range(L):
            S.op("act", (lambda a: lambda e: e.dma_start(out=g1T_all[:, a, :], in_=norm1_d[a, :].rearrange("(kc p) -> p kc", p=128),
                                                         allow_slow_non_contiguous=True))(l_), writes=[B_gall], ndma=1)
            S.op("act", (lambda a: lambda e: e.dma_start(out=g2T_all[:, a, :], in_=norm2_d[a, :].rearrange("(kc p) -> p kc", p=128),
                                                         allow_slow_non_contiguous=True))(l_), writes=[B_gall], ndma=1)
        def dma(eng, out_ap, in_ap, reads, writes, slow=False):
            if slow:
                return S.op(eng, lambda e: e.dma_start(out=out_ap, in_=in_ap, allow_slow_non_contiguous=True),
                            reads=reads, writes=writes, ndma=1)
            return S.op(eng, lambda e: e.dma_start(out=out_ap, in_=in_ap), reads=reads, writes=writes, ndma=1)

        def mm(out_ap, lhsT, rhs, start, stop, reads, writes):
            return S.op("pe", lambda e: e.matmul(out_ap, lhsT=lhsT, rhs=rhs, start=start, stop=stop),
                        reads=reads, writes=writes)

        def tr(out_ap, in_ap, reads, writes):
            return S.op("pe", lambda e: e.transpose(out_ap, in_ap, ident[:]), reads=list(reads) + [B_ident], writes=writes)

        def act(out_ap, in_ap, func, reads, writes, scale=None, bias=None, accum=None):
            def f(e):
                kw = {}
                if scale is not None:
                    kw["scale"] = scale
                if bias is not None:
                    kw["bias"] = bias
                if accum is not None:
                    kw["accum_out"] = accum
                return e.activation(out=out_ap, in_=in_ap, func=func, **kw)
            return S.op("act", f, reads=reads, writes=writes)

        def tt(eng, out_ap, in0, in1, op, reads, writes):
            return S.op(eng, lambda e: e.tensor_tensor(out=out_ap, in0=in0, in1=in1, op=op), reads=reads, writes=writes)

        def ts(eng, out_ap, in0, s1, s2, op0, op1, reads, writes):
            if s2 is None:
                return S.op(eng, lambda e: e.tensor_scalar(out=out_ap, in0=in0, scalar1=s1, scalar2=None, op0=op0),
                            reads=reads, writes=writes)
            return S.op(eng, lambda e: e.tensor_scalar(out=out_ap, in0=in0, scalar1=s1, scalar2=s2, op0=op0, op1=op1),
                        reads=reads, writes=writes)

        def stt(eng, out_ap, in0, scalar, in1, op0, op1, reads, writes):
            return S.op(eng, lambda e: e.scalar_tensor_tensor(out=out_ap, in0=in0, scalar=scalar, in1=in1, op0=op0, op1=op1),
                        reads=reads, writes=writes)

        def cp(eng, out_ap, in_ap, reads, writes):
            return S.op(eng, lambda e: e.tensor_copy(out=out_ap, in_=in_ap), reads=reads, writes=writes)

        def rstd(out_ap, ss_ap, tmp_ap, n, w, B_ss, B_tmp, B_out):
            ts("dve", tmp_ap, ss_ap, 1.0 / n, EPS, ALU.mult, ALU.add, [B_ss], [B_tmp])
            tt("pool", out_ap, tmp_ap, cneg[:, 0:w], ALU.pow, [B_tmp, B_cneg], [B_out])

        B_modrow = [Buf() for _ in range(L)]
        B_wfi_bf = [[Buf() for _ in range(11)] for _ in range(L)]
        B_wfo_bf = [[Buf() for _ in range(2)] for _ in range(L)]
        MH = 256

        def mod_load(lay, j, adab_t, B_adab_t, brow_t, B_brow_t):
            dma("pool", adab_t[:], w_ada_d[lay, :, j * MH:(j + 1) * MH].rearrange("(kc p) n -> p kc n", p=128), [], [B_adab_t])
            dma("act", brow_t[:], b_ada_d[lay, j * MH:(j + 1) * MH].partition_broadcast(128), [], [B_brow_t])

        def mod_compute(lay, j, adab_t, B_adab_t, brow_t, B_brow_t, mrow_t, B_mrow_t):
            pb, B_pb = pring.next()
            for kc in range(KC):
                mm(pb[:, 0:MH], scB[:, kc, :], adab_t[:, kc, :], kc == 0, kc == KC - 1, [B_scB, B_adab_t], [B_pb])
            tt("dve", mrow_t[:], pb[:, 0:MH], brow_t[:], ALU.add, [B_pb, B_brow_t], [B_mrow_t])
            dma("sp", mod_d[lay:lay + 1, j * MH:(j + 1) * MH], mrow_t[0:1, :], [B_mrow_t], [B_modrow[lay]])

        def mod_tr(lay, j, mrow_t, B_mrow_t):
            nseg = MH // 128
            col = j * MH
            m_, kc_ = col // D, (col % D) // 128
            for sg in range(nseg):
                tt("dve", dscr[:], mrow_t[:, sg * 128:(sg + 1) * 128], identF[:], ALU.mult, [B_mrow_t, B_identF], [B_dscr])
                S.op("dve", (lambda o_, i_: lambda e: e.tensor_reduce(out=o_, in_=i_, axis=AX.X, op=ALU.add))(
                    modT_all[:, lay, m_, kc_ + sg:kc_ + sg + 1], dscr[:]), reads=[B_dscr], writes=[B_modTa[lay]])

        def mod_block(lay, j, adab_t, B_adab_t, brow_t, B_brow_t, mrow_t, B_mrow_t):
            mod_load(lay, j, adab_t, B_adab_t, brow_t, B_brow_t)
            mod_compute(lay, j, adab_t, B_adab_t, brow_t, B_brow_t, mrow_t, B_mrow_t)

        NMB = 6 * D // MH
        with ExitStack() as st1_:
            crow = sbt(st1_, "crow", [1, D], F32)
            B_crow = Buf()
            S.op("sp", lambda e: e.dma_start(out=crow[:], in_=c_d[0:1, :]), writes=[B_crow], ndma=1)
            pcT, B_pcT = pring.next()
            for kc_ in range(KC):
                S.op("pe", (lambda k_: lambda e: e.matmul(pcT[:, k_:k_ + 1], lhsT=crow[0:1, k_ * 128:(k_ + 1) * 128],
                                                           rhs=one_f[0:1, 0:1], start=True, stop=True))(kc_),
                     reads=[B_crow, B_onef], writes=[B_pcT])
            S.op("act", lambda e: e.activation(out=scT[:], in_=pcT[:, 0:KC], func=AF.Silu), reads=[B_pcT], writes=[B_scT])
            S.op("dve", lambda e: e.tensor_copy(out=scB[:], in_=scT[:].unsqueeze(2).to_broadcast([128, KC, 128])), reads=[B_scT], writes=[B_scB])
            S.op("dve", lambda e: e.tensor_copy(out=identF[:], in_=ident[:]), reads=[B_ident], writes=[B_identF])
            estage = sbt(st1_, "estage", [128, 3, 2, 512], F32)
            B_es = Buf()
            S.op("sp", lambda e: e.dma_start(out=estage[:], in_=abias_d), writes=[B_es], ndma=1)
            for ri_ in range(3):
                for hh_ in range(2):
                    S.op("act", (lambda a, b: lambda e: e.activation(out=Etab[:, a, b, :], in_=estage[:, a, b, :], func=AF.Exp))(ri_, hh_),
                         reads=[B_es], writes=[B_E])
            adab0 = [sbt(st1_, f"adab0_{i}", [128, KC, MH], BF16) for i in range(2)]
            brow0 = [sbt(st1_, f"brow0_{i}", [128, MH], F32) for i in range(2)]
            mrow0 = [sbt(st1_, f"mrow0_{i}", [128, MH], F32) for i in range(2)]
            Bq = [[Buf(), Buf(), Buf()] for _ in range(2)]
            mod_load(0, 0, adab0[0], Bq[0][0], brow0[0], Bq[0][1])
            for j in range(NMB):
                r = j % 2
                if j + 1 < NMB:
                    mod_load(0, j + 1, adab0[1 - r], Bq[1 - r][0], brow0[1 - r], Bq[1 - r][1])
                mod_compute(0, j, adab0[r], Bq[r][0], brow0[r], Bq[r][1], mrow0[r], Bq[r][2])
                if j >= 1:
                    mod_tr(0, j - 1, mrow0[1 - r], Bq[1 - r][2])
            mod_tr(0, NMB - 1, mrow0[(NMB - 1) % 2], Bq[(NMB - 1) % 2][2])
            S.barrier()

        for l in range(L):
            x_src = x_d if l == 0 else xb_d
            B_src = dram_x["x"] if l == 0 else dram_x["xb"]
            x_mid, B_mid = xa_d, dram_x["xa"]
            if l == L - 1:
                x_dst, B_dst = out_d, dram_x["out"]
            else:
                x_dst, B_dst = xb_d, dram_x["xb"]

            with ExitStack() as ph:
                win = sbt(ph, "win", [128, KC, INW], BF16)
                wa = sbt(ph, "wa", [128, 4, D], BF16)
                wb = sbt(ph, "wb", [128, 4, D], BF16)
                wo = sbt(ph, "wo", [128, KC, D], BF16)
                wsT = sbt(ph, "wsT", [128, 8, 128], BF16)
                modT = sbt(ph, "modT", [128, 6, KC], F32)
                g1T = sbt(ph, "g1T", [128, KC], F32)
                gm1 = sbt(ph, "gm1", [128, KC], F32)
                vgain = sbt(ph, "vgain", [128, 512], F32)
                gq = sbt(ph, "gq", [128, 64], F32)
                gk = sbt(ph, "gk", [128, 64], F32)
                gqk = sbt(ph, "gqk", [128, 64], F32)
                bsT = sbt(ph, "bsT", [128, 4, 128], F32)
                sinkE = sbt(ph, "sinkE", [128, 8], F32)
                B_win = [Buf() for _ in range(8)]
                B_wa, B_wb, B_wo, B_wsT, B_modT = Buf(), Buf(), Buf(), Buf(), Buf()
                B_g1T, B_gm1, B_vgain, B_gq, B_gk, B_gqk, B_bsT, B_sinkE, B_gt1b = (Buf() for _ in range(9))

                wsn = sbt(ph, "wsn", [128, 8, 128], BF16)
                gt1b = sbt(ph, "gt1b", [128, D], F32)
                B_wsn = Buf()
                cp("dve", modT[:], modT_all[:, l, :, :], [B_modTa[l]], [B_modT])
                stt("dve", gm1[:], modT[:, 1, :], 1.0, g1T_all[:, l, :], ALU.add, ALU.mult, [B_modT, B_gall], [B_gm1])
                dma("sp", gq[:], qg_d[l, :].partition_broadcast(128), [], [B_gq])
                dma("sp", gk[:], kg_d[l, :].partition_broadcast(128), [], [B_gk])
                stt("dve", gqk[:], gq[:], 0.125, gk[:], ALU.mult, ALU.mult, [B_gq, B_gk], [B_gqk])
                dma("sp", vgain[:], vg_d[l, :].partition_broadcast(128), [], [B_vgain])
                dma("sp", sinkE[:], sink_d[l, :].partition_broadcast(128), [], [B_sinkE])
                act(sinkE[:], sinkE[:], AF.Exp, [B_sinkE], [B_sinkE])
                for hh in range(2):
                    dma("sp", bsT[hh * 64:(hh + 1) * 64, :, :], bs_d[l, hh::2, :].unsqueeze(0).to_broadcast([64, 4, 128]), [], [B_bsT])
                dma("sp", gt1b[:], mod_d[l, 2 * D:3 * D].partition_broadcast(128), [B_modrow[l]], [B_gt1b])
                for j in (3, 1, 2, 0, 4, 5, 6, 7):
                    c0 = j * 512
                    c1 = min(INW, c0 + 512)
                    dma("pool", win[:, :, c0:c1], w_in_d[l, :, c0:c1].rearrange("(kc p) n -> p kc n", p=128), [], [B_win[j]])
                dma("pool", wsn[:], ws_d[l].rearrange("g t s -> t g s"), [], [B_wsn])
                dma("pool", wa[:], wa_d[l].rearrange("(kc p) n -> p kc n", p=128), [], [B_wa])
                dma("pool", wb[:], wb_d[l].rearrange("(kc p) n -> p kc n", p=128), [], [B_wb])
                dma("pool", wo[:], wo_d[l].rearrange("(kc p) n -> p kc n", p=128), [], [B_wo])

                def preconvert_ffn(part):
                    if part < 11:
                        j = part
                        dma("pool", wfi_bf_d[l, :, j * 512:(j + 1) * 512], wfi_d[l, :, j * 512:(j + 1) * 512], [], [B_wfi_bf[l][j]])
                    else:
                        hf = part - 11
                        dma("pool", wfo_bf_d[l, hf * 1408:(hf + 1) * 1408, :], wfo_d[l, hf * 1408:(hf + 1) * 1408, :], [], [B_wfo_bf[l][hf]])

                def setup_wsT():
                    pt, B_pt = ptring.next()
                    for g_ in range(8):
                        tr(pt[:, g_ * 128:(g_ + 1) * 128], wsn[:, g_, :], [B_wsn], [B_pt])
                    cp("dve", wsT[:], pt[:, 0:1024].rearrange("p (g t) -> p g t", t=128), [B_pt], [B_wsT])

                def setup_wo():
                    for kc in range(KC):
                        stt("dve", wo[:, kc, :], wo[:, kc, :], 0.5, gt1b[:], ALU.mult, ALU.mult, [B_wo, B_gt1b], [B_wo])

                NHT = 4
                W = GM * 128
                xt = Ring([(sbt(ph, f"xt{i}", [128, D], F32), Buf()) for i in range(2)])
                xs = Ring([(sbt(ph, f"xs{i}", [128, D], BF16), Buf()) for i in range(2)])
                hT = sbt(ph, "hT", [128, KC, NHT * 128], BF16)
                B_hT = [Buf() for _ in range(NHT)]
                uT = sbt(ph, "uT", [128, 4, W], BF16)
                B_uT = Buf()
                sgT = sbt(ph, "sgT", [128, 16, W], BF16)
                B_sgT = [Buf() for _ in range(16)]
                gv = Ring([(sbt(ph, f"gv{i}", [128, 512], BF16), Buf()) for i in range(2)])
                vn = Ring([(sbt(ph, f"vn{i}", [128, 512], BF16), Buf()) for i in range(2)])
                sq = Ring([(sbt(ph, f"sq{i}", [128, 512], F32), Buf()) for i in range(2)])
                qn = Ring([(sbt(ph, f"qn{i}", [128, 512], BF16), Buf()) for i in range(2)])
                ksq = Ring([(sbt(ph, f"ksq{i}", [128, 128], F32), Buf()) for i in range(2)])
                ktmp = Ring([(sbt(ph, f"ktmp{i}", [128, 128], F32), Buf()) for i in range(2)])
                kdup = Ring([(sbt(ph, f"kdup{i}", [128, 256], BF16), Buf()) for i in range(2)])
                qT = sbt(ph, "qT", [128, 4, W], BF16)
                B_qT = [Buf() for _ in range(GM)]
                kT = sbt(ph, "kT", [128, 2, 8 * 128], BF16)
                B_kT = [Buf() for _ in range(8)]
                va = sbt(ph, "va", [128, 8, 2, 65], BF16)
                B_va = [Buf() for _ in range(8)]
                aT = sbt(ph, "aT", [128, 4, W], BF16)
                B_aT = [Buf() for _ in range(GM)]
                oT = sbt(ph, "oT", [128, 4, W], BF16)
                B_oT = [Buf() for _ in range(GM)]
                otok = Ring([(sbt(ph, f"otok{i}", [128, 512], BF16), Buf()) for i in range(2)])
                gtmp = Ring([(sbt(ph, f"gtmp{i}", [128, 512], BF16), Buf()) for i in range(2)])
                expP = Ring([(sbt(ph, f"expP{i}", [128, 512], BF16), Buf()) for i in range(2)])
                PTr = Ring([(sbt(ph, f"PT{i}", [128, 3, 2, 512], BF16), [[Buf() for _ in range(2)] for _ in range(3)]) for i in range(2)])
                tmpA = sbt(ph, "tmpA", [128, KC, W], BF16)
                B_tmpA = [Buf() for _ in range(KC)]
                tmpB = Ring([(sbt(ph, f"tmpB{i}", [128, W], BF16), Buf()) for i in range(2)])
                mT = sbt(ph, "mT", [128, KC, W], BF16)
                B_mT = [Buf() for _ in range(KC)]
                xo = Ring([(sbt(ph, f"xo{i}", [128, D], F32), Buf()) for i in range(2)])
                NSL = 8
                st1 = sbt(ph, "st1", [128, NSL, 8], F32)
                stq = sbt(ph, "stq", [128, NSL, 3, 8], F32)
                stk = sbt(ph, "stk", [128, NSL, 3, 2], F32)
                dn = Ring([(sbt(ph, f"dn{i}", [128, 8], F32), Buf()) for i in range(2)])
                B_st = [[Buf() for _ in range(12)] for _ in range(NT)]
                B_st = [B_st[t_ % 8] for t_ in range(NT)]
                S.op("pool", (lambda v_: lambda e: e.memset(v_[:, :, :, 64:65], 1.0))(va), writes=B_va)

                e_state = {}

                def e_load(t):
                    xtt, B_xt = xt.next()
                    dma("sp", xtt[:], x_src[t * 128:(t + 1) * 128, :], [B_src[t]], [B_xt])
                    e_state[("x", t)] = (xtt, B_xt)

                def e_step_a(t):
                    if ("x", t) not in e_state:
                        e_load(t)
                    xtt, B_xt = e_state.pop(("x", t))
                    xst, B_xs = xs.next()
                    act(xst[:], xtt[:], AF.Square, [B_xt], [B_st[t][0], B_xs], accum=st1[:, t % 8, 0:1])
                    rstd(st1[:, t % 8, 2:3], st1[:, t % 8, 0:1], st1[:, t % 8, 1:2], D, 1, B_st[t][0], B_st[t][1], B_st[t][2])
                    ts("dve", xst[:], xtt[:], st1[:, t % 8, 2:3], None, ALU.mult, None, [B_xt, B_st[t][2]], [B_xs])
                    e_state[t] = (xst, B_xs)

                def e1(t):
                    s4 = t % NHT
                    xst, B_xs = e_state.pop(t)
                    pt, B_pt = ptring.next()
                    for kc in range(KC):
                        tr(pt[:, kc * 128:(kc + 1) * 128], xst[:, kc * 128:(kc + 1) * 128], [B_xs], [B_pt])
                    for kc in range(KC):
                        ts("dve", hT[:, kc, s4 * 128:(s4 + 1) * 128], pt[:, kc * 128:(kc + 1) * 128],
                           gm1[:, kc:kc + 1], modT[:, 0, kc:kc + 1], ALU.mult, ALU.add,
                           [B_pt, B_gm1, B_modT], [B_hT[s4]])

                def e2(t):
                    s4 = t % NHT
                    s8 = t % 8
                    pb, B_pb = pring.next()
                    for kc in range(KC):
                        mm(pb[:, 0:256], hT[:, kc, s4 * 128:(s4 + 1) * 128], win[:, kc, K0:K0 + 256], kc == 0, kc == KC - 1,
                           [B_hT[s4], B_win[3]], [B_pb])
                    act(va[:, s8, :, 0:64], pb[:, 128:256].rearrange("p (kv d) -> p kv d", d=64), AF.Copy, [B_pb], [B_va[s8]])
                    ks, B_ks = ksq.next()
                    act(ks[:], pb[:, 0:128], AF.Square, [B_pb], [B_ks])
                    S.op("dve", (lambda o_, i_: lambda e: e.tensor_reduce(out=o_, in_=i_, axis=AX.X, op=ALU.add))(
                        stk[:, t % 8, 0, :], ks[:].rearrange("p (h d) -> p h d", d=64)), reads=[B_ks], writes=[B_st[t][6]])
                    rstd(stk[:, t % 8, 2, :], stk[:, t % 8, 0, :], stk[:, t % 8, 1, :], 64, 2, B_st[t][6], B_st[t][7], B_st[t][8])
                    kt_, B_kt = ktmp.next()
                    tt("dve", kt_[:].rearrange("p (h d) -> p h d", d=64), pb[:, 0:128].rearrange("p (h d) -> p h d", d=64),
                       stk[:, t % 8, 2, :].unsqueeze(2).to_broadcast([128, 2, 64]), ALU.mult, [B_pb, B_st[t][8]], [B_kt])
                    kd, B_kd = kdup.next()
                    for dup in range(2):
                        tt("dve", kd[:].rearrange("p (kv u d) -> p kv u d", kv=2, u=2)[:, :, dup, :],
                           kt_[:].rearrange("p (h d) -> p h d", d=64),
                           gqk[:].unsqueeze(1).to_broadcast([128, 2, 64]), ALU.mult, [B_kt, B_gqk], [B_kd])
                    e_state[("k", t)] = (kd, B_kd)

                def e3(t):
                    s8 = t % 8
                    kd, B_kd = e_state.pop(("k", t))
                    pt2, B_pt2 = ptring.next()
                    for kv in range(2):
                        tr(pt2[:, kv * 128:(kv + 1) * 128], kd[:, kv * 128:(kv + 1) * 128], [B_kd], [B_pt2])
                    cp("dve", kT[:, :, s8 * 128:(s8 + 1) * 128], pt2[:, 0:256].rearrange("p (kv t) -> p kv t", t=128), [B_pt2], [B_kT[s8]])

                def tok_mm(t, i):
                    s4 = t % NHT
                    pb, B_pb = pring.next()
                    for kc in range(KC):
                        mm(pb[:], hT[:, kc, s4 * 128:(s4 + 1) * 128], win[:, kc, V0:V0 + 512], kc == 0, kc == KC - 1,
                           [B_hT[s4], B_win[1]], [B_pb])
                    gvt, B_gv = gv.next()
                    act(gvt[:], pb[:], AF.Gelu_apprx_tanh, [B_pb], [B_gv])
                    vnt, B_vn = vn.next()
                    act(vnt[:], gvt[:], AF.Square, [B_gv], [B_st[t][3], B_vn], accum=st1[:, t % 8, 3:4])
                    rstd(st1[:, t % 8, 5:6], st1[:, t % 8, 3:4], st1[:, t % 8, 4:5], 512, 1, B_st[t][3], B_st[t][4], B_st[t][5])
                    stt("dve", vnt[:], gvt[:], st1[:, t % 8, 5:6], vgain[:], ALU.mult, ALU.mult, [B_gv, B_st[t][5], B_vgain], [B_vn])
                    pq, B_pq = pring.next()
                    for kc in range(KC):
                        mm(pq[:], hT[:, kc, s4 * 128:(s4 + 1) * 128], win[:, kc, Q0:Q0 + 512], kc == 0, kc == KC - 1,
                           [B_hT[s4], B_win[2]], [B_pq])
                    sqt, B_sq = sq.next()
                    act(sqt[:], pq[:], AF.Square, [B_pq], [B_sq])
                    S.op("dve", (lambda o_, i_: lambda e: e.tensor_reduce(out=o_, in_=i_, axis=AX.X, op=ALU.add))(
                        stq[:, t % 8, 0, :], sqt[:].rearrange("p (h d) -> p h d", d=64)), reads=[B_sq], writes=[B_st[t][9]])
                    rstd(stq[:, t % 8, 2, :], stq[:, t % 8, 0, :], stq[:, t % 8, 1, :], 64, 8, B_st[t][9], B_st[t][10], B_st[t][11])
                    qnt, B_qn = qn.next()
                    tt("dve", qnt[:].rearrange("p (h d) -> p h d", d=64), pq[:].rearrange("p (h d) -> p h d", d=64),
                       stq[:, t % 8, 2, :].unsqueeze(2).to_broadcast([128, 8, 64]), ALU.mult, [B_pq, B_st[t][11]], [B_qn])
                    return (vnt, B_vn, qnt, B_qn)

                def q_tr(i, qnt, B_qn):
                    pt, B_pt = ptring.next()
                    for j in range(4):
                        tr(pt[:, j * 128:(j + 1) * 128], qnt[:, j * 128:(j + 1) * 128], [B_qn], [B_pt])
                    cp("dve", qT[:, :, i * 128:(i + 1) * 128], pt[:, 0:512].rearrange("p (j t) -> p j t", t=128), [B_pt], [B_qT[i]])

                def gmlp(i, vnt, B_vn):
                    pg, B_pg = pring.next()
                    for j in range(4):
                        for hh in range(2):
                            g = 2 * j + hh
                            mm(pg[hh * 64:(hh + 1) * 64, j * 128:(j + 1) * 128], vnt[:, g * 64:(g + 1) * 64], wsT[:, g, :],
                               True, True, [B_vn, B_wsT], [B_pg])
                    gt_, B_gt = gtmp.next()
                    tt("dve", gt_[:], pg[:], bsT[:].rearrange("p j t -> p (j t)"), ALU.add, [B_pg, B_bsT], [B_gt])
                    tt("pool", aT[:, :, i * 128:(i + 1) * 128], gt_[:].rearrange("p (j t) -> p j t", t=128),
                       uT[:, :, i * 128:(i + 1) * 128], ALU.mult, [B_gt, B_uT], [B_aT[i]])

                def attn_logits(t, i, filler=()):
                    rels = [r_ for r_ in (-1, 0, 1) if 0 <= t + r_ < NT]
                    PT, B_PT = PTr.next()
                    filler = list(filler)
                    for r_ in rels:
                        if r_ != rels[0] and filler:
                            c_wa(*filler.pop(0))
                        ri = r_ + 1
                        s8r = (t + r_) % 8
                        for hh in range(2):
                            pb, B_pb = pring.next()
                            for kv in range(2):
                                mm(pb[:, kv * 256:(kv + 1) * 256].rearrange("p (j t) -> p j t", t=128),
                                   kT[hh * 64:(hh + 1) * 64, kv, s8r * 128:(s8r + 1) * 128],
                                   qT[hh * 64:(hh + 1) * 64, kv * 2:kv * 2 + 2, i * 128:(i + 1) * 128],
                                   True, True, [B_kT[s8r], B_qT[i]], [B_pb])
                            ex, B_ex = expP.next()
                            act(ex[:], pb[:], AF.Exp, [B_pb], [B_ex])
                            tt("dve", PT[:, ri, hh, :], ex[:], Etab[:, ri, hh, :], ALU.mult, [B_ex, B_E], [B_PT[ri][hh]])
                    while filler:
                        c_wa(*filler.pop(0))
                    return (rels, PT, B_PT)

                def attn_pv(t, i, rels, PT, B_PT):
                    banks = [pring.next(), pring.next()]
                    for h in range(8):
                        j, hh = divmod(h, 2)
                        kv = j // 2
                        po, B_po = banks[h // 4]
                        c0 = (h % 4) * 65
                        for idx, r_ in enumerate(rels):
                            ri = r_ + 1
                            s8r = (t + r_) % 8
                            mm(po[:, c0:c0 + 65], PT[:, ri, hh, j * 128:(j + 1) * 128], va[:, s8r, kv, :],
                               idx == 0, idx == len(rels) - 1, [B_PT[ri][hh], B_va[s8r]], [B_po])
                    dnt, B_dn = dn.next()
                    for bi in range(2):
                        po, B_po = banks[bi]
                        tt("dve", dnt[:, bi * 4:(bi + 1) * 4].unsqueeze(2),
                           po[:, 0:260].rearrange("p (h e) -> p h e", e=65)[:, :, 64:65],
                           sinkE[:, bi * 4:(bi + 1) * 4].unsqueeze(2), ALU.add, [B_po, B_sinkE], [B_dn])
                    S.op("dve", lambda e: e.reciprocal(out=dnt[:], in_=dnt[:]), reads=[B_dn], writes=[B_dn])
                    ot, B_ot = otok.next()
                    for bi in range(2):
                        po, B_po = banks[bi]
                        tt("dve", ot[:, bi * 256:(bi + 1) * 256].rearrange("p (h d) -> p h d", d=64),
                           po[:, 0:260].rearrange("p (h e) -> p h e", e=65)[:, :, 0:64],
                           dnt[:, bi * 4:(bi + 1) * 4].unsqueeze(2).to_broadcast([128, 4, 64]), ALU.mult,
                           [B_po, B_dn], [B_ot])
                    return (ot, B_ot)

                def attn_ot(i, ot, B_ot):
                    pt, B_pt = ptring.next()
                    for j in range(4):
                        tr(pt[:, j * 128:(j + 1) * 128], ot[:, j * 128:(j + 1) * 128], [B_ot], [B_pt])
                    cp("dve", oT[:, :, i * 128:(i + 1) * 128], pt[:, 0:512].rearrange("p (j t) -> p j t", t=128), [B_pt], [B_oT[i]])

                def feat_chunk(g, kind, cc):
                    t0 = g * GM
                    s0 = (t0 % NHT) * 128
                    rb = [B_hT[(t0 + i) % NHT] for i in range(GM)]
                    c0 = (U0 if kind == "u" else GA0) + cc * 128
                    pb, B_pb = pring.next()
                    for kc in range(KC):
                        mm(pb[:, 0:W], win[:, kc, c0:c0 + 128], hT[:, kc, s0:s0 + W],
                           kc == 0, kc == KC - 1, rb + [B_win[c0 // 512]], [B_pb])
                    if kind == "u":
                        act(uT[:, cc, :], pb[:, 0:W], AF.Gelu_apprx_tanh, [B_pb], [B_uT])
                    else:
                        act(sgT[:, cc, :], pb[:, 0:W], AF.Tanh, [B_pb], [B_sgT[cc]], scale=0.5)

                def c_wa(d0, d1):
                    for dc in range(d0, d1):
                        pa, B_pa = pring.next()
                        for cc in range(4):
                            mm(pa[:, 0:W], wa[:, cc, dc * 128:(dc + 1) * 128], aT[:, cc, :], cc == 0, cc == 3, B_aT + [B_wa], [B_pa])
                        stt("dve", tmpA[:, dc, :], sgT[:, dc, :], 1.0, pa[:, 0:W], ALU.add, ALU.mult, [B_pa, B_sgT[dc]], [B_tmpA[dc]])

                def c_wb():
                    for dc in range(KC):
                        pb_, B_pb_ = pring.next()
                        for j in range(4):
                            mm(pb_[:, 0:W], wb[:, j, dc * 128:(dc + 1) * 128], oT[:, j, :], j == 0, j == 3, B_oT + [B_wb], [B_pb_])
                        tb, B_tb = tmpB.next()
                        stt("dve", tb[:], sgT[:, 8 + dc, :], 1.0, pb_[:, 0:W], ALU.add, ALU.mult, [B_pb_, B_sgT[8 + dc]], [B_tb])
                        tt("pool", mT[:, dc, :], tmpA[:, dc, :], tb[:], ALU.add, [B_tmpA[dc], B_tb], [B_mT[dc]])

                def c_wo(g):
                    t0 = g * GM
                    loads = []
                    for i in range(GM):
                        t = t0 + i
                        xot, B_xo = xo.next()
                        dma("sp", xot[:], x_src[t * 128:(t + 1) * 128, :], [B_src[t]], [B_xo])
                        loads.append((xot, B_xo))
                    for i in range(GM):
                        t = t0 + i
                        xot, B_xo = loads[i]
                        for hc in range(2):
                            po, B_po = pring.next()
                            for dc in range(KC):
                                mm(po[:], mT[:, dc, i * 128:(i + 1) * 128], wo[:, dc, hc * 512:(hc + 1) * 512], dc == 0, dc == KC - 1,
                                   B_mT + [B_wo], [B_po])
                            tt("dve", xot[:, hc * 512:(hc + 1) * 512], po[:], xot[:, hc * 512:(hc + 1) * 512], ALU.add,
                               [B_po, B_xo], [B_xo])
                        dma("sp", x_mid[t * 128:(t + 1) * 128, :], xot[:], [B_xo], [B_mid[t]])

                NG = NT // GM
                for t in range(GM):
                    e_step_a(t)
                    e1(t)
                    e2(t)
                    e3(t)
                toks = [tok_mm(i, i) for i in range(GM)]
                for cc in range(4):
                    feat_chunk(0, "u", cc)
                for g in range(NG):
                    t0 = g * GM
                    nxt = [t for t in (t0 + GM, t0 + GM + 1) if t < NT]
                    n0 = nxt[0] if len(nxt) > 0 else None
                    n1 = nxt[1] if len(nxt) > 1 else None
                    for t in nxt:
                        e_step_a(t)
                    for cc in range(8):
                        feat_chunk(g, "g", cc)
                    for i in range(GM):
                        q_tr(i, toks[i][2], toks[i][3])
                    if n0 is not None:
                        e1(n0)
                    for cc in range(8, 16):
                        feat_chunk(g, "g", cc)
                    if g == 0:
                        setup_wsT()
                    for i in range(GM):
                        gmlp(i, toks[i][0], toks[i][1])
                    if n0 is not None:
                        e2(n0)
                    if n1 is not None:
                        e1(n1)
                    lg0 = attn_logits(t0, 0, filler=[(0, 2), (2, 4)])
                    if n0 is not None:
                        e3(n0)
                    lg1 = attn_logits(t0 + 1, 1, filler=[(4, 6), (6, 8)])
                    if n1 is not None:
                        e2(n1)
                    o0 = attn_pv(t0, 0, *lg0)
                    o1 = attn_pv(t0 + 1, 1, *lg1)
                    for t_ in (t0 + 2 * GM, t0 + 2 * GM + 1):
                        if t_ < NT:
                            e_load(t_)
                    ntoks = []
                    if n0 is not None:
                        ntoks.append(tok_mm(n0, 0))
                    attn_ot(0, *o0)
                    attn_ot(1, *o1)
                    if n1 is not None:
                        ntoks.append(tok_mm(n1, 1))
                        e3(n1)
                    c_wb()
                    for part_ in range(13):
                        if (NG >= 15 and g == 1 + part_) or (NG < 15 and g == 0):
                            preconvert_ffn(part_)
                    if n0 is not None:
                        for cc in range(4):
                            feat_chunk(g + 1, "u", cc)
                    if g == 0:
                        setup_wo()
                    c_wo(g)
                    toks = ntoks
                S.barrier()

            with ExitStack() as ph:
                wfi = sbt(ph, "wfi", [128, KC, 2 * DFF], BF16)
                wfo = sbt(ph, "wfo", [128, NHC, D], BF16)
                modT2 = sbt(ph, "modT2", [128, 6, KC], F32)
                g2T = sbt(ph, "g2Tb", [128, KC], F32)
                gm2 = sbt(ph, "gm2", [128, KC], F32)
                gt2b = sbt(ph, "gt2b", [128, D], F32)
                B_wfi = [Buf() for _ in range(11)]
                B_wfo = [Buf() for _ in range(2)]
                B_modT2, B_g2T, B_gm2, B_gt2b = (Buf() for _ in range(4))
                cp("dve", modT2[:], modT_all[:, l, :, :], [B_modTa[l]], [B_modT2])
                stt("dve", gm2[:], modT2[:, 4, :], 1.0, g2T_all[:, l, :], ALU.add, ALU.mult, [B_modT2, B_gall], [B_gm2])
                dma("sp", gt2b[:], mod_d[l, 5 * D:6 * D].partition_broadcast(128), [B_modrow[l]], [B_gt2b])
                order = []
                for hcx in range(NHC):
                    for c_ in (hcx * 128, DFF + hcx * 128):
                        if c_ // 512 not in order:
                            order.append(c_ // 512)
                for j in order:
                    dma("pool", wfi[:, :, j * 512:(j + 1) * 512], wfi_bf_d[l, :, j * 512:(j + 1) * 512].rearrange("(kc p) n -> p kc n", p=128), [B_wfi_bf[l][j]], [B_wfi[j]])
                for hf in range(2):
                    dma("pool", wfo[:, hf * 11:(hf + 1) * 11, :], wfo_bf_d[l, hf * 11 * 128:(hf + 1) * 11 * 128, :].rearrange("(kc p) n -> p kc n", p=128), [B_wfo_bf[l][hf]], [B_wfo[hf]])

                def setup_wfo():
                    for kk in range(NHC):
                        tt("dve", wfo[:, kk, :], wfo[:, kk, :], gt2b[:], ALU.mult, [B_wfo[kk // 11], B_gt2b], [B_wfo[kk // 11]])

                W = GF * 128
                NGF = NT // GF
                xt = Ring([(sbt(ph, f"fxt{i}", [128, D], F32), Buf()) for i in range(2)])
                xs = Ring([(sbt(ph, f"fxs{i}", [128, D], BF16), Buf()) for i in range(2)])
                hTs = [sbt(ph, f"fhT{i}", [128, KC, W], BF16) for i in range(2)]
                B_hTs = [[Buf() for _ in range(GF)] for _ in range(2)]
                hid = sbt(ph, "hid", [128, NHC, W], BF16)
                B_hid = [Buf() for _ in range(NHC)]
                sl = Ring([(sbt(ph, f"sl{i}", [128, W], BF16), Buf()) for i in range(2)])
                xo = Ring([(sbt(ph, f"fxo{i}", [128, D], F32), Buf()) for i in range(2)])
                st2 = sbt(ph, "st2", [128, NT, 4], F32)
                B_st2 = [[Buf() for _ in range(3)] for _ in range(NT)]
                adab = [sbt(ph, f"adab_{i}", [128, KC, MH], BF16) for i in range(2)]
                browf = [sbt(ph, f"brow_{i}", [128, MH], F32) for i in range(2)]
                mrowf = [sbt(ph, f"mrow_{i}", [128, MH], F32) for i in range(2)]
                Bq = [[Buf(), Buf(), Buf()] for _ in range(2)]
                mod_next = list(range(NMB)) if l + 1 < L else []
                mod_loaded = []
                mod_done = []
                f_state = {}

                def prep_a(t):
                    xtt, B_xt = xt.next()
                    dma("sp", xtt[:], x_mid[t * 128:(t + 1) * 128, :], [B_mid[t]], [B_xt])
                    xst, B_xs = xs.next()
                    act(xst[:], xtt[:], AF.Square, [B_xt], [B_st2[t][0], B_xs], accum=st2[:, t, 0:1])
                    rstd(st2[:, t, 2:3], st2[:, t, 0:1], st2[:, t, 1:2], D, 1, B_st2[t][0], B_st2[t][1], B_st2[t][2])
                    ts("dve", xst[:], xtt[:], st2[:, t, 2:3], None, ALU.mult, None, [B_xt, B_st2[t][2]], [B_xs])
                    f_state[t] = (xst, B_xs)

                def prep_b(t, par, i):
                    xst, B_xs = f_state.pop(t)
                    pt, B_pt = ptring.next()
                    for kc in range(KC):
                        tr(pt[:, kc * 128:(kc + 1) * 128], xst[:, kc * 128:(kc + 1) * 128], [B_xs], [B_pt])
                    for kc in range(KC):
                        ts("dve", hTs[par][:, kc, i * 128:(i + 1) * 128], pt[:, kc * 128:(kc + 1) * 128],
                           gm2[:, kc:kc + 1], modT2[:, 3, kc:kc + 1], ALU.mult, ALU.add,
                           [B_pt, B_gm2, B_modT2], [B_hTs[par][i]])

                for i in range(GF):
                    prep_a(i)
                    prep_b(i, 0, i)
                a_at = {1: 0, 11: 1}
                b_at = {6: 0, 16: 1}
                for g in range(NGF):
                    t0 = g * GF
                    par = g % 2
                    hT = hTs[par]
                    B_hT = B_hTs[par]
                    loads = {}
                    for hcx in range(NHC):
                        pg, B_pg = pring.next()
                        c0 = hcx * 128
                        for kc in range(KC):
                            mm(pg[:, 0:W], wfi[:, kc, c0:c0 + 128], hT[:, kc, :], kc == 0, kc == KC - 1, B_hT + [B_wfi[c0 // 512]], [B_pg])
                        slt, B_sl = sl.next()
                        act(slt[:], pg[:, 0:W], AF.Silu, [B_pg], [B_sl])
                        pu, B_pu = pring.next()
                        c1 = DFF + hcx * 128
                        for kc in range(KC):
                            mm(pu[:, 0:W], wfi[:, kc, c1:c1 + 128], hT[:, kc, :], kc == 0, kc == KC - 1, B_hT + [B_wfi[c1 // 512]], [B_pu])
                        tt("dve", hid[:, hcx, :], pu[:, 0:W], slt[:], ALU.mult, [B_pu, B_sl], [B_hid[hcx]])
                        if g + 1 < NGF:
                            if hcx in a_at:
                                prep_a(t0 + GF + a_at[hcx])
                            if hcx in b_at:
                                prep_b(t0 + GF + b_at[hcx], 1 - par, b_at[hcx])
                        if hcx in (4, 14) and (mod_next or mod_loaded or mod_done) and g >= 1:
                            if mod_done:
                                j = mod_done.pop(0)
                                mod_tr(l + 1, j, mrowf[j % 2], Bq[j % 2][2])
                            if mod_loaded:
                                j = mod_loaded.pop(0)
                                r = j % 2
                                mod_compute(l + 1, j, adab[r], Bq[r][0], browf[r], Bq[r][1], mrowf[r], Bq[r][2])
                                mod_done.append(j)
                            if mod_next:
                                j = mod_next.pop(0)
                                r = j % 2
                                mod_load(l + 1, j, adab[r], Bq[r][0], browf[r], Bq[r][1])
                                mod_loaded.append(j)
                        if hcx == NHC - 2:
                            for i in range(GF):
                                t = t0 + i
                                xot, B_xo = xo.next()
                                dma("sp", xot[:], x_mid[t * 128:(t + 1) * 128, :], [B_mid[t]], [B_xo])
                                loads[i] = (xot, B_xo)
                    if g == 0:
                        setup_wfo()
                    for i in range(GF):
                        t = t0 + i
                        xot, B_xo = loads[i]
                        for hc in range(2):
                            po, B_po = pring.next()
                            for kk in range(NHC):
                                mm(po[:], hid[:, kk, i * 128:(i + 1) * 128], wfo[:, kk, hc * 512:(hc + 1) * 512], kk == 0, kk == NHC - 1,
                                   B_hid + B_wfo, [B_po])
                            tt("dve", xot[:, hc * 512:(hc + 1) * 512], po[:], xot[:, hc * 512:(hc + 1) * 512], ALU.add,
                               [B_po, B_xo], [B_xo])
                        dma("sp", x_dst[t * 128:(t + 1) * 128, :], xot[:], [B_xo], [B_dst[t]])
                while mod_next or mod_loaded or mod_done:
                    if mod_done:
                        j = mod_done.pop(0)
                        mod_tr(l + 1, j, mrowf[j % 2], Bq[j % 2][2])
                    if mod_loaded:
                        j = mod_loaded.pop(0)
                        r = j % 2
                        mod_compute(l + 1, j, adab[r], Bq[r][0], browf[r], Bq[r][1], mrowf[r], Bq[r][2])
                        mod_done.append(j)
                    if mod_next:
                        j = mod_next.pop(0)
                        r = j % 2
                        mod_load(l + 1, j, adab[r], Bq[r][0], browf[r], Bq[r][1])
                        mod_loaded.append(j)
                S.barrier()
        S.emit(nc, top)
    return nc


def alibi_bias_table():
    s = np.arange(128)[:, None]
    t = np.arange(128)[None, :]
    tab = np.zeros((128, 3, 2, 4, 128), np.float32)
    for ri, rel in enumerate((-1, 0, 1)):
        dist = np.abs(t - s - 128 * rel).astype(np.float32)
        for hh in range(2):
            for kvjj in range(4):
                kv, jj = divmod(kvjj, 2)
                h = kv * 4 + jj * 2 + hh
                slope = 2.0 ** (-8.0 * (h + 1) / 8.0)
                b = -slope * dist
                b = np.where(dist <= 128, b, NEG)
                tab[:, ri, hh, kvjj, :] = b
    return tab.reshape(128, 3, 2, 512)


_PROG = {}
WEIGHT_KEYS = ["w_ada", "b_ada", "norm1_g", "w_in", "gm_v_g", "gm_w_s", "gm_b_s", "q_norm_g", "k_norm_g",
               "attn_sink", "w_a", "w_b", "w_o", "norm2_g", "w_ffn_in", "w_ffn_out"]
N_LAYERS_PER_LAUNCH = 4


def kernel(**inputs):
    x = np.ascontiguousarray(inputs["x"], dtype=np.float32)
    c = np.ascontiguousarray(inputs["c"], dtype=np.float32)
    Bsz, T, _ = x.shape
    depth = inputs["w_ada"].shape[0]
    Lp = N_LAYERS_PER_LAUNCH
    key = (T, Lp)
    if key not in _PROG:
        _PROG[key] = build_program(T, Lp)
    nc = _PROG[key]
    abias = alibi_bias_table()
    cur = [x[b] for b in range(Bsz)]
    for l0 in range(0, depth, Lp):
        w = {k: np.ascontiguousarray(inputs[k][l0:l0 + Lp], dtype=np.float32) for k in WEIGHT_KEYS}
        in_maps = []
        for b in range(Bsz):
            m = {"x": cur[b], "c": c[b:b + 1], "abias": abias}
            m.update(w)
            in_maps.append(m)
        res = run_bass_kernel_spmd(nc, in_maps, core_ids=list(range(Bsz)))
        cur = [np.asarray(res.results[b]["out"], dtype=np.float32) for b in range(Bsz)]
    return np.stack(cur, axis=0)
```
